# Optimizing a Trainium2 kernel written in Bass

```python
import jax, jax.numpy as jnp
from jax import lax
import numpy as np

D_MODEL = 1024
BATCH = 4
SEQ = 4096
DEPTH = 4

HEAD_DIM = 64
ATTN_HEADS = D_MODEL // 2 // HEAD_DIM
RWKV_HEADS = D_MODEL // 2 // HEAD_DIM
ATTN_WIDTH = ATTN_HEADS * HEAD_DIM
RWKV_WIDTH = RWKV_HEADS * HEAD_DIM
ATTN_BRANCHES = ((128, 1), (512, 4), (2048, 16))
DECAY_LORA = 64
AAA_LORA = 64
GATE_LORA = 128
RWKV_PROJ = 3 * RWKV_WIDTH + DECAY_LORA + AAA_LORA + GATE_LORA
IN_PROJ = 3 * ATTN_WIDTH + RWKV_PROJ
D_FF = 4 * D_MODEL
NORM_EPS = 1e-6
GN_EPS = HEAD_DIM * 1e-5
NEG_INF = -1e30

kernel_name = 'hybrid_dilated_attn_rwkv7_encoder'


def _rms_norm(x, g):
    xf = x.astype(jnp.float32)
    y = xf * lax.rsqrt(jnp.mean(xf * xf, axis=-1, keepdims=True) + NORM_EPS)
    return (y * g.astype(jnp.float32)).astype(x.dtype)


def _alibi_slopes(n):
    return jnp.exp2(-8.0 * jnp.arange(1, n + 1, dtype=jnp.float32) / n)


def _strided_band_attention(q, k, v, slopes, radius, dilation):
    B, S, H, HD = q.shape
    L = S // dilation
    N = B * dilation
    blk = radius
    nb = -(-L // blk)
    Lp = nb * blk

    def to_sub(t):
        return t.reshape(B, L, dilation, H, HD).transpose(0, 2, 1, 3, 4).reshape(N, L, H, HD)

    qb = jnp.pad(to_sub(q), ((0, 0), (0, Lp - L), (0, 0), (0, 0))).reshape(N, nb, blk, H, HD)

    def band(t):
        tp = jnp.pad(to_sub(t), ((0, 0), (blk, Lp - L + blk), (0, 0), (0, 0))).reshape(N, nb + 2, blk, H, HD)
        return jnp.concatenate([tp[:, :-2], tp[:, 1:-1], tp[:, 2:]], axis=2)

    kb, vb = band(k), band(v)
    s = jnp.einsum('nbqhd,nbkhd->nbhqk', qb, kb, preferred_element_type=jnp.float32)
    q_idx = jnp.arange(nb)[:, None] * blk + jnp.arange(blk)[None, :]
    k_idx = jnp.arange(nb)[:, None] * blk - blk + jnp.arange(3 * blk)[None, :]
    dist = jnp.abs(k_idx[:, None, :] - q_idx[:, :, None])
    valid = (dist <= radius) & (k_idx[:, None, :] >= 0) & (k_idx[:, None, :] < L)
    alibi = -slopes[None, :, None, None] * (dilation * dist).astype(jnp.float32)[:, None]
    s = jnp.where(valid[:, None], s + alibi, NEG_INF)
    m = jnp.max(s, axis=-1)
    p = jnp.exp(s - m[..., None])
    l = jnp.sum(p, axis=-1)
    o = jnp.einsum('nbhqk,nbkhd->nbqhd', p, vb.astype(jnp.float32)) / jnp.swapaxes(l, 2, 3)[..., None]

    def from_sub(t):
        t = t.reshape((N, Lp) + t.shape[3:])[:, :L]
        t = t.reshape((B, dilation, L) + t.shape[2:])
        return jnp.swapaxes(t, 1, 2).reshape((B, S) + t.shape[3:])

    return from_sub(o), from_sub(jnp.swapaxes(m, 2, 3)), from_sub(jnp.swapaxes(l, 2, 3))


def _dilated_window_attention(q, k, v, q_g, k_g):
    B, S, _ = q.shape
    shp = (B, S, ATTN_HEADS, HEAD_DIM)
    q = _rms_norm(q.reshape(shp), q_g) * (HEAD_DIM ** -0.5)
    k = _rms_norm(k.reshape(shp), k_g)
    v = v.reshape(shp)
    slopes = _alibi_slopes(ATTN_HEADS)
    outs, maxes, sums = [], [], []
    for window, dilation in ATTN_BRANCHES:
        o, m, l = _strided_band_attention(q, k, v, slopes, window // (2 * dilation), dilation)
        outs.append(o)
        maxes.append(m)
        sums.append(l)
    o = jnp.stack(outs)
    m = jnp.stack(maxes)
    l = jnp.stack(sums)
    wts = l * jnp.exp(m - jnp.max(m, axis=0, keepdims=True))
    wts = wts / jnp.sum(wts, axis=0, keepdims=True)
    out = jnp.sum(wts[..., None] * o, axis=0)
    return out.reshape(B, S, ATTN_WIDTH).astype(v.dtype)


def _wkv7_scan(r, w, k, v, kk, a, reverse):
    B, S, H, N = r.shape
    xs = tuple(jnp.moveaxis(t, 1, 0) for t in (r, w, k, v, kk, a))

    def step(state, inp):
        r_t, w_t, k_t, v_t, kk_t, a_t = inp
        s_kk = jnp.einsum('bhvk,bhk->bhv', state, kk_t)
        state = (state * w_t[:, :, None, :]
                 - s_kk[..., None] * (kk_t * a_t)[:, :, None, :]
                 + v_t[..., None] * k_t[:, :, None, :])
        return state, jnp.einsum('bhvk,bhk->bhv', state, r_t)

    init = jnp.zeros((B, H, N, N), jnp.float32)
    _, ys = lax.scan(step, init, xs, reverse=reverse)
    return jnp.moveaxis(ys, 0, 1)


def _rwkv7_bidirectional(u, shift_prev, shift_next, w0, w2, a0, a2, g2, k_k, k_a, r_k, gn_w, gn_b):
    B, S, _ = u.shape
    u_prev = jnp.pad(u, ((0, 0), (1, 0), (0, 0)))[:, :-1]
    u_next = jnp.pad(u, ((0, 0), (0, 1), (0, 0)))[:, 1:]
    u = (u + shift_prev * (u_prev - u) + shift_next * (u_next - u)).astype(jnp.float32)
    c1 = RWKV_WIDTH
    c4 = 3 * RWKV_WIDTH + DECAY_LORA
    r, k, v, xw, xa, xg = jnp.split(u, [c1, 2 * c1, 3 * c1, c4, c4 + AAA_LORA], axis=-1)
    shp = (B, S, RWKV_HEADS, HEAD_DIM)

    def heads(t):
        return t.reshape(shp)

    g = jax.nn.sigmoid(xg) @ g2
    kk = heads(k * k_k)
    kk = kk * lax.rsqrt(jnp.sum(kk * kk, axis=-1, keepdims=True) + 1e-12)
    tw = jnp.tanh(xw)
    ys, kts = [], []
    for direction, reverse in ((0, False), (1, True)):
        wlog = -jax.nn.softplus(-(w0[direction] + tw @ w2[direction])) - 0.5
        decay = jnp.exp(-jnp.exp(wlog))
        a = jax.nn.sigmoid(a0[direction] + xa @ a2[direction])
        kt = k * (1.0 + (a - 1.0) * k_a)
        ys.append(_wkv7_scan(heads(r), heads(decay), heads(kt), heads(v), kk, heads(a), reverse))
        kts.append(kt)
    y = ys[0] + ys[1]
    mu = jnp.mean(y, axis=-1, keepdims=True)
    var = jnp.mean(jnp.square(y - mu), axis=-1, keepdims=True)
    y = ((y - mu) * lax.rsqrt(var + GN_EPS)).reshape(B, S, RWKV_WIDTH) * gn_w + gn_b
    bonus = jnp.sum(heads(r) * heads(kts[0] + kts[1]) * r_k, axis=-1, keepdims=True) * heads(v)
    return (y + bonus.reshape(B, S, RWKV_WIDTH)) * g


def setup_inputs(seed: int = 0) -> dict:
    key = jax.random.key(seed)
    ks = jax.random.split(key, 21)
    f = jnp.float32
    L = DEPTH

    def nrm(k, shape, scale):
        return jax.random.normal(k, shape, f) * scale

    return {
        'x': nrm(ks[0], (BATCH, SEQ, D_MODEL), 1.0),
        'ln1_g': 1.0 + nrm(ks[1], (L, D_MODEL), 0.02),
        'w_in': nrm(ks[2], (L, D_MODEL, IN_PROJ), D_MODEL ** -0.5),
        'q_norm_g': 1.0 + nrm(ks[3], (L, HEAD_DIM), 0.02),
        'k_norm_g': 1.0 + nrm(ks[4], (L, HEAD_DIM), 0.02),
        'tshift_prev': jax.random.uniform(ks[5], (L, RWKV_PROJ), f, 0.0, 0.5),
        'tshift_next': jax.random.uniform(ks[6], (L, RWKV_PROJ), f, 0.0, 0.5),
        'rwkv_w0': jax.random.uniform(ks[7], (L, 2, RWKV_WIDTH), f, -4.0, 1.0),
        'rwkv_w2': nrm(ks[8], (L, 2, DECAY_LORA, RWKV_WIDTH), 0.5 * DECAY_LORA ** -0.5),
        'rwkv_a0': nrm(ks[9], (L, 2, RWKV_WIDTH), 0.1),
        'rwkv_a2': nrm(ks[10], (L, 2, AAA_LORA, RWKV_WIDTH), 0.5 * AAA_LORA ** -0.5),
        'rwkv_g2': nrm(ks[11], (L, GATE_LORA, RWKV_WIDTH), GATE_LORA ** -0.5),
        'rwkv_k_k': 0.85 + nrm(ks[12], (L, RWKV_WIDTH), 0.02),
        'rwkv_k_a': 1.0 + nrm(ks[13], (L, RWKV_WIDTH), 0.02),
        'rwkv_r_k': nrm(ks[14], (L, RWKV_HEADS, HEAD_DIM), 0.1),
        'rwkv_gn_w': 1.0 + nrm(ks[15], (L, RWKV_WIDTH), 0.02),
        'rwkv_gn_b': nrm(ks[16], (L, RWKV_WIDTH), 0.02),
        'w_out': nrm(ks[17], (L, D_MODEL, D_MODEL), 0.5 * D_MODEL ** -0.5),
        'ln2_g': 1.0 + nrm(ks[18], (L, D_MODEL), 0.02),
        'w_up': nrm(ks[19], (L, D_MODEL, D_FF), D_MODEL ** -0.5),
        'w_down': nrm(ks[20], (L, D_FF, D_MODEL), 0.5 * D_FF ** -0.5),
    }


def reference(x, ln1_g, w_in, q_norm_g, k_norm_g, tshift_prev, tshift_next, rwkv_w0, rwkv_w2,
              rwkv_a0, rwkv_a2, rwkv_g2, rwkv_k_k, rwkv_k_a, rwkv_r_k, rwkv_gn_w, rwkv_gn_b,
              w_out, ln2_g, w_up, w_down):
    for layer in range(DEPTH):
        h = _rms_norm(x, ln1_g[layer])
        proj = h @ w_in[layer]
        q = proj[..., :ATTN_WIDTH]
        k = proj[..., ATTN_WIDTH:2 * ATTN_WIDTH]
        v = proj[..., 2 * ATTN_WIDTH:3 * ATTN_WIDTH]
        u = proj[..., 3 * ATTN_WIDTH:]
        attn = _dilated_window_attention(q, k, v, q_norm_g[layer], k_norm_g[layer])
        rwkv = _rwkv7_bidirectional(u, tshift_prev[layer], tshift_next[layer], rwkv_w0[layer],
                                    rwkv_w2[layer], rwkv_a0[layer], rwkv_a2[layer], rwkv_g2[layer],
                                    rwkv_k_k[layer], rwkv_k_a[layer], rwkv_r_k[layer],
                                    rwkv_gn_w[layer], rwkv_gn_b[layer]).astype(x.dtype)
        x = x + jnp.concatenate([attn, rwkv], axis=-1) @ w_out[layer]
        h = _rms_norm(x, ln2_g[layer])
        x = x + jnp.square(jax.nn.relu(h @ w_up[layer])) @ w_down[layer]
    return x
```

```python
import contextlib
import concourse.bass as bass
import concourse.mybir as mybir

F32 = mybir.dt.float32
BF16 = mybir.dt.bfloat16
I32 = mybir.dt.int32
AF = mybir.ActivationFunctionType
ALU = mybir.AluOpType
AX = mybir.AxisListType

ENGS = ("pe", "act", "dve", "pool", "sp")


class Reg:
    __slots__ = ("name", "w", "r")

    def __init__(self, name=""):
        self.name = name
        self.w = None
        self.r = []


class Tile:
    def __init__(self, t, name):
        self.t = t
        self.reg = Reg(name)

    def __getitem__(self, idx):
        return self.t[idx]


class _Rec:
    def __init__(self):
        self.calls = []

    def __getattr__(self, name):
        def f(*a, **k):
            self.calls.append((name, a, k))
            return self
        return f


def _capture(fn):
    rec = _Rec()
    fn(rec)
    assert len(rec.calls) == 1, rec.calls
    name, a, k = rec.calls[0]
    return lambda e: getattr(e, name)(*a, **k)


class Sched:
    def __init__(self, nc, n_dma_sems=24, strict_same=True):
        self.nc = nc
        self.stack = contextlib.ExitStack()
        self.strict_same = strict_same
        import os as _os
        self.tailwait = _os.environ.get("TAILWAIT", "1") != "0"
        self.prog = {e: [] for e in ENGS}
        self.sems = []
        self.eng_sem = {}
        self.cnt = {}
        for e in ENGS:
            self.eng_sem[e] = self._new_sem("s_" + e)
            self.cnt[e] = 0
        self.dma_ring = {}
        self.dma_pos = {}
        self.dma_val = {}
        for e in ("sp", "pool", "act"):
            ring = [self._new_sem("d_%s_%d" % (e, i)) for i in range(n_dma_sems)]
            self.dma_ring[e] = ring
            self.dma_pos[e] = 0
        self.semval = [0] * len(self.sems)
        self.observed = {e: {} for e in ENGS}
        self.ninst = 0
        self.final_toks = []
        self.waited = {}

    def _new_sem(self, name):
        s = self.stack.enter_context(self.nc.semaphore(name))
        self.sems.append(s)
        return len(self.sems) - 1

    def sb(self, shape, dtype, name):
        t = self.stack.enter_context(self.nc.sbuf_tensor(name, list(shape), dtype))
        return Tile(t, name)

    def ps(self, shape, dtype, name):
        t = self.stack.enter_context(self.nc.psum_tensor(name, list(shape), dtype))
        return Tile(t, name)

    def _need(self, eng, tok, waits):
        if tok is None:
            return
        sem, val, teng = tok
        if teng == eng and (eng == "pe" or not self.strict_same) and sem == self.eng_sem[eng]:
            return
        if self.observed[eng].get(sem, 0) >= val:
            return
        if self.tailwait and teng in self.cnt and teng != eng:
            val = self.cnt[teng]
        if waits.get(sem, 0) < val:
            waits[sem] = val

    def _deps(self, eng, r, w):
        waits = {}
        for reg in r:
            self._need(eng, reg.w, waits)
        for reg in w:
            self._need(eng, reg.w, waits)
            for tok in reg.r:
                self._need(eng, tok, waits)
        return waits

    def _emit_waits(self, eng, waits):
        for sem, val in waits.items():
            self.observed[eng][sem] = val
            self.prog[eng].append(("wait", sem, val))
            self.waited.setdefault(sem, set()).add(val)

    def _mark(self, tok, r, w):
        for reg in r:
            reg.r.append(tok)
        for reg in w:
            reg.w = tok
            reg.r = []

    def _regs(self, lst):
        out = []
        for x in lst:
            if isinstance(x, Tile):
                out.append(x.reg)
            elif isinstance(x, Reg):
                out.append(x)
            elif x is None:
                continue
            else:
                out.extend(self._regs(x))
        return out

    def op(self, eng, fn, r=(), w=()):
        fn = _capture(fn)
        r = self._regs(r)
        w = self._regs(w)
        waits = self._deps(eng, r, w)
        self._emit_waits(eng, waits)
        sem = self.eng_sem[eng]
        self.cnt[eng] += 1
        val = self.cnt[eng]
        s = self.sems[sem]
        self.prog[eng].append(("op", fn, sem, val))
        tok = (sem, val, eng)
        self._mark(tok, r, w)
        self.ninst += 1
        return tok

    def dma(self, eng, fn, r=(), w=(), final=False):
        fn = _capture(fn)
        r = self._regs(r)
        w = self._regs(w)
        waits = self._deps(eng, r, w)
        ring = self.dma_ring[eng]
        pos = self.dma_pos[eng]
        self.dma_pos[eng] = (pos + 1) % len(ring)
        sem = ring[pos]
        prev = self.semval[sem]
        if prev > 0 and self.observed[eng].get(sem, 0) < prev:
            if waits.get(sem, 0) < prev:
                waits[sem] = prev
        self._emit_waits(eng, waits)
        val = prev + 16
        self.semval[sem] = val
        s = self.sems[sem]
        self.prog[eng].append(("dma", fn, sem, val))
        tok = (sem, val, "dma_" + eng)
        self._mark(tok, r, w)
        self.ninst += 1
        if final:
            self.final_toks.append(tok)
        return tok

    def barrier(self, engs=("pe", "act", "dve", "pool")):
        for e in engs:
            waits = {}
            for o in engs:
                if o == e:
                    continue
                sem = self.eng_sem[o]
                val = self.cnt[o]
                if val > 0 and self.observed[e].get(sem, 0) < val:
                    waits[sem] = val
            self._emit_waits(e, waits)

    def wait_all(self, eng, regs):
        regs = self._regs(regs)
        waits = {}
        for reg in regs:
            if reg.w is not None:
                sem, val, teng = reg.w
                if self.observed[eng].get(sem, 0) < val and waits.get(sem, 0) < val:
                    waits[sem] = val
        self._emit_waits(eng, waits)

    def finish(self):
        nc = self.nc
        prog = self.prog
        waits = {}
        for sem, val, _ in self.final_toks:
            if self.observed["sp"].get(sem, 0) < val and waits.get(sem, 0) < val:
                waits[sem] = val
        self._emit_waits("sp", waits)
        eng_sems = set(self.eng_sem.values())
        idx = {}
        for sem in eng_sems:
            vals = sorted(self.waited.get(sem, ()))
            idx[sem] = {v: i + 1 for i, v in enumerate(vals)}
        sems = self.sems
        self.n_inc = sum(len(v) for v in idx.values())

        def replay(e, items):
            for it in items:
                if it[0] == "wait":
                    _, sem, val = it
                    if sem in eng_sems:
                        val = idx[sem][val]
                    e.wait_ge(sems[sem], val)
                elif it[0] == "op":
                    _, fn, sem, val = it
                    ins = fn(e)
                    if val in idx[sem]:
                        ins.then_inc(sems[sem], 1)
                else:
                    _, fn, sem, val = it
                    fn(e).then_inc(sems[sem], 16)

        with nc.Block() as block:
            @block.tensor
            def _(e):
                replay(e, prog["pe"])

            @block.scalar
            def _(e):
                replay(e, prog["act"])

            @block.vector
            def _(e):
                replay(e, prog["dve"])

            @block.gpsimd
            def _(e):
                replay(e, prog["pool"])

            @block.sync
            def _(e):
                replay(e, prog["sp"])
        self.stack.close()

import numpy as np
import os

C = 128
SBK = 512
NCH = SBK // C
E05 = float(np.exp(-0.5))
GN_EPS = 64 * 1e-5
import os
STAGE = int(os.environ.get('STAGE', '99'))
SUB = os.environ.get('SUB', 'ab')
SUB5 = os.environ.get('SUB5', 'qgh')
SUB6 = os.environ.get('SUB6', 'abc')

CV = dict(spr=0, snr=1, c0r=2, spk=3, snk=4, c0k=5, spv=6, snv=7, c0v=8, kk=9, ka=10, rk=11,
          w00=12, w01=13, a00=14, a01=15)
NCV = 16
NXV = 3


DBG = {}
def dbg(nc, S, name, tile, ap_fn, shape, dtype=None):
    if os.environ.get("DBG") is None or name in DBG:
        return
    if os.environ.get("DBG") == "f32" and dtype is not None:
        return
    if os.environ.get("DBG") not in ("1", "f32") and name not in os.environ.get("DBG").split(","):
        return
    dt = dtype or F32
    d = nc.dram_tensor("dbg_" + name, list(shape), dt, kind="ExternalOutput").ap()
    DBG[name] = d
    S.dma("sp", lambda e: e.dma_start(out=d, in_=ap_fn()), r=[tile], final=True)


def emit_rwkv(nc, S, T, d_in, d_out, consts, NJ=2, dirs=(0, 1)):
    u_rkv = d_in["u_rkv"]
    u_x = d_in["u_x"]
    cvec = d_in["cvec"]
    xvec = d_in["xvec"]
    wa2 = d_in["wa2"]
    g2 = d_in["g2"]
    gnwb = d_in["gnwb"]
    yacc_d = d_in["yacc"]
    nsb = T // SBK
    ident = consts["ident"]
    identf = consts["identf"]

    cv = S.sb([128, NJ, NCV], F32, "cv")
    xv = S.sb([128, 2, NXV], F32, "xv")
    wa2_sb = S.sb([128, 4, 128 * NJ], BF16, "wa2_sb")
    g2_sb = S.sb([128, 128 * NJ], BF16, "g2_sb")
    gnwb_sb = S.sb([128, 2, 128 * NJ], F32, "gnwb_sb")
    S.dma("sp", lambda e: e.dma_start(out=cv[:, :, :], in_=cvec[:, :, :]), w=[cv])
    S.dma("sp", lambda e: e.dma_start(out=xv[:, :, :], in_=xvec[:, :, :]), w=[xv])
    S.dma("pool", lambda e: e.dma_start(out=wa2_sb[:, :, :], in_=wa2[:, :, :]), w=[wa2_sb])
    S.dma("pool", lambda e: e.dma_start(out=g2_sb[:, :], in_=g2[:, :]), w=[g2_sb])
    S.dma("sp", lambda e: e.dma_start(out=gnwb_sb[:, :, :], in_=gnwb[:, :, :]), w=[gnwb_sb])

    for _i in range(int(os.environ.get("DUMMY", "0"))):
        S.dma("sp", lambda e: e.dma_start(out=xv[:, :, :], in_=xvec[:, :, :]), w=[xv])
    ones_f = S.sb([128, 256], F32, "ones_f")
    S.op("pool", lambda e: e.memset(ones_f[:, :], 1.0), w=[ones_f])
    bones = S.sb([128, 128], BF16, "bones")
    S.op("pool", lambda e: e.memset(bones[:, :], 0.0), w=[bones])
    S.op("pool", lambda e: e.memset(bones[0:64, 0:64], 1.0), w=[bones])
    S.op("pool", lambda e: e.memset(bones[64:128, 64:128], 1.0), w=[bones])
    mk = {}
    for name, cm, st, cmp in (("SF", -1, 1, ALU.is_gt), ("IF", -1, 1, ALU.is_ge), ("SR", 1, -1, ALU.is_gt), ("IR", 1, -1, ALU.is_ge)):
        m = S.sb([128, 128], F32, "mk_" + name)
        S.op("pool", lambda e, m=m, cm=cm, st=st, cmp=cmp: e.affine_select(out=m[:, :], in_=ones_f[:, 0:128], pattern=[[st, 128]], compare_op=cmp, fill=0.0, base=0, channel_multiplier=cm), r=[ones_f], w=[m])
        mk[name] = m
    mk2 = []
    for d, (a_, b_) in enumerate((("SF", "IF"), ("SR", "IR"))):
        m = S.sb([128, 256], F32, "mk2_%d" % d)
        S.op("dve", lambda e, m=m, a_=a_: e.tensor_copy(out=m[:, 0:128], in_=mk[a_][:, :]), r=[mk[a_]], w=[m])
        S.op("dve", lambda e, m=m, b_=b_: e.tensor_copy(out=m[:, 128:256], in_=mk[b_][:, :]), r=[mk[b_]], w=[m])
        mk2.append(m)
    mkL = [mk["SR"], mk["SF"]]
    eps12 = S.sb([128, 1], F32, "eps12")
    S.op("pool", lambda e: e.memset(eps12[:, :], 1e-12), w=[eps12])
    epsgn = S.sb([128, 1], F32, "epsgn")
    S.op("pool", lambda e: e.memset(epsgn[:, :], GN_EPS), w=[epsgn])
    rmask = S.sb([128, SBK], F32, "rmask")
    S.op("pool", lambda e: e.memset(rmask[:, :], 1.0), w=[rmask])
    for c in range(NCH):
        S.op("pool", lambda e, c=c: e.memset(rmask[:, c * C:c * C + 1], 0.0), w=[rmask])

    W = SBK + 2
    ux = [S.sb([128, W], F32, "ux%d" % i) for i in range(2)]
    urkv = [[S.sb([128, W], F32, "urkv%d_%d" % (q, j)) for j in range(2)] for q in range(3)]
    tmp = S.sb([128, SBK], F32, "tmp")
    dbgt = S.sb([128, SBK], F32, "dbgt")
    dbgt2 = S.sb([128, SBK], F32, "dbgt2")
    dbgt3 = S.sb([128, SBK], F32, "dbgt3")
    xs = [S.sb([128, SBK], F32, "xs%d" % i) for i in range(2)]
    xb = [S.sb([128, SBK], BF16, "xb%d" % i) for i in range(2)]
    rkv = [[S.sb([128, SBK], F32, "rkv%d_%d" % (q, j)) for j in range(2)] for q in range(3)]
    kkt = [S.sb([128, SBK], F32, "kk%d" % j) for j in range(2)]
    sqb = S.sb([128, SBK], BF16, "sqb")
    a_t = [[S.sb([128, SBK], F32, "a%d_%d" % (d, j)) for j in range(2)] for d in range(2)]
    lw = [S.sb([128, SBK], F32, "lw%d" % j) for j in range(2)]
    cp = [S.sb([128, SBK], F32, "cp%d" % j) for j in range(2)]
    ce = [S.sb([128, SBK], F32, "ce%d" % j) for j in range(2)]
    kt = [S.sb([128, SBK], F32, "kt%d" % j) for j in range(2)]
    kts = [S.sb([128, SBK], F32, "kts%d" % j) for j in range(2)]
    akk = [S.sb([128, SBK], F32, "akk%d" % j) for j in range(2)]
    ex = [S.sb([128, SBK], F32, "ex%d" % i) for i in range(4)]
    fm = {n: [S.sb([128, SBK], BF16, "fm_%s%d" % (n, j)) for j in range(2)] for n in ("At", "Bt", "Bpt", "Kt", "Kpt", "Rt", "Vt")}
    fmo = {n: [S.sb([64, SBK], BF16, "fmo_%s%d" % (n, j)) for j in range(2)] for n in ("At", "Bt", "Bpt", "Kt", "Kpt", "Rt", "Vt")}
    PCo = [S.sb([64, NCH], F32, "PCo%d" % j) for j in range(2)]
    g_t = [S.sb([128, SBK], F32, "g%d" % j) for j in range(2)]
    PCc = [S.sb([128, NCH], F32, "PCc%d" % j) for j in range(2)]
    yacc_reg = [Reg("yacc%d" % i) for i in range(T // C)]
    bon = [S.sb([128, SBK], F32, "bon%d" % j) for j in range(2)]

    ps_prep = [S.ps([128, 512], F32, "ps_prep%d" % i) for i in range(2)]
    bankP0 = [S.ps([128, 512], F32, "bankP0_%d" % e) for e in range(2)]
    _bp1 = S.ps([128, 512], F32, "bankP1")
    bankD = S.ps([128, 512], F32, "bankD")
    bankS0 = S.ps([128, 512], F32, "bankS0")
    bankS1 = S.ps([128, 512], F32, "bankS1")
    def sub(bank, lo, hi, name):
        return (bank, lo, hi, Reg(name))
    PS1 = [sub(bankP0[e], 0, 256, "PS1") for e in range(2)]
    PS2 = [sub(bankP0[e], 256, 512, "PS2") for e in range(2)]
    PS3 = [sub(_bp1, 256 * e, 256 * e + 128, "PS3") for e in range(2)]
    PS4 = [sub(_bp1, 256 * e + 128, 256 * e + 256, "PS4") for e in range(2)]
    PSY = [sub(bankD, 64 * e, 64 * e + 64, "PSY") for e in range(2)]
    PS5 = sub(bankS0, 0, 64, "PS5")
    PSz = sub(bankS0, 64, 192, "PSz")
    PSn = sub(bankS1, 0, 128, "PSn")
    PSl = sub(bankS1, 128, 256, "PSl")
    PSq = sub(bankS0, 192, 320, "PSq")
    PSg = sub(bankS0, 320, 384, "PSg")
    PSh = sub(bankS1, 256, 320, "PSh")
    _pss = sub(bankD, 128, 192, "PSs")
    PSs = [_pss, _pss]
    PStr = sub(bankD, 192, 320, "PStr")

    def pa(s_, p0=0, p1=128, lo=None, hi=None):
        bank, a, b, _ = s_
        lo = a if lo is None else a + lo
        hi = b if hi is None else a + hi
        return bank.t[p0:p1, lo:hi]

    X1 = [S.sb([128, 256], BF16, "X1_%d" % e) for e in range(2)]
    X2 = [S.sb([128, 256], BF16, "X2_%d" % e) for e in range(2)]
    Lm = [[S.sb([128, 128], BF16, "L_%d_%d" % (e, i)) for i in range(2)] for e in range(2)]
    Nm = [[S.sb([128, 128], BF16, "N_%d_%d" % (e, i)) for i in range(2)] for e in range(2)]
    TM = [S.sb([128, 256], BF16, "TM_%d" % e) for e in range(2)]
    Z = [[S.sb([128, 128], BF16, "Z_%d_%d" % (e, i)) for i in range(2)] for e in range(2)]
    Lmk = [[S.sb([128, 128], BF16, "Lmk_%d_%d" % (e, lv)) for lv in range(7)] for e in range(2)]
    XT = [S.sb([128, 128], BF16, "XT_%d" % e) for e in range(2)]
    Tbuf = [[S.sb([128, 128], BF16, "Tb_%d_%d" % (e, i)) for i in range(2)] for e in range(2)]
    TTbuf = [[S.sb([128, 128], BF16, "TbT_%d_%d" % (e, i)) for i in range(2)] for e in range(2)]
    lvm = S.sb([128, 7, 2, 128], F32, "lvm_sb")
    S.dma("sp", lambda e: e.dma_start(out=lvm[:, :, :, :], in_=d_in["lvm"][:, :, :, :]), w=[lvm])
    Qt = [S.sb([128, 128], BF16, "Qt_%d" % e) for e in range(2)]
    Gt = [S.sb([128, 128], BF16, "Gt_%d" % e) for e in range(2)]
    for e in range(2):
        S.op("pool", lambda e_, e=e: e_.memset(Gt[e][:, :], 0.0), w=[Gt[e]])
        S.op("pool", lambda e_, e=e: e_.memset(Qt[e][:, :], 0.0), w=[Qt[e]])
    Hs = [S.sb([128, 64], F32, "Hs_%d" % e) for e in range(2)]
    Sth = [[[S.sb([128, 64], BF16, "St_%d_%d_%d" % (j, i, e)) for e in range(2)] for i in range(2)] for j in range(2)]
    ytok = S.sb([128, 128 * NJ], F32, "ytok")
    ytok_reg = [Reg("ytok%d" % h) for h in range(2 * NJ)]
    yprev = S.sb([128, 128 * NJ], F32, "yprev")
    stats = S.sb([128, 2 * NJ, 6], F32, "stats")
    mv = S.sb([128, 2 * NJ, 2], F32, "mv")
    rstd = S.sb([128, 2 * NJ], F32, "rstd")
    yn = S.sb([128, 128 * NJ], F32, "yn")
    yo = [S.sb([128, SBK], F32, "yo%d" % j) for j in range(2)]

    def shift(dst, src, vt, vi, cols, eng="dve"):
        sp_, sn_, c0_ = cols
        vec = vt.t[:, vi, :]
        S.op("dve", lambda e: e.tensor_scalar(out=tmp[:, :], in0=src[:, 0:SBK], scalar1=vec[:, sp_:sp_ + 1], scalar2=None, op0=ALU.mult), r=[src, vt], w=[tmp])
        S.op("dve", lambda e: e.scalar_tensor_tensor(out=tmp[:, :], in0=src[:, 2:SBK + 2], scalar=vec[:, sn_:sn_ + 1], in1=tmp[:, :], op0=ALU.mult, op1=ALU.add), r=[src, vt, tmp], w=[tmp])
        S.op("dve", lambda e: e.scalar_tensor_tensor(out=dst[:, :], in0=src[:, 1:SBK + 1], scalar=vec[:, c0_:c0_ + 1], in1=tmp[:, :], op0=ALU.mult, op1=ALU.add), r=[src, vt, tmp], w=[dst])

    NH = 2 * NJ
    CW = 128 * NJ
    for d in list(dirs) * int(os.environ.get('NREP', '1')):
        sb_order = list(range(nsb)) if d == 0 else list(range(nsb - 1, -1, -1))
        for j in range(NJ):
            for e in range(2):
                S.op("pool", lambda e_, j=j, e=e: e_.memset(Sth[j][0][e][:, :], 0.0), w=[Sth[j][0][e]])
                S.op("pool", lambda e_, j=j, e=e: e_.memset(Sth[j][1][e][:, :], 0.0), w=[Sth[j][1][e]])
        cur = [[0, 0], [0, 0]]
        for sb in sb_order:
            t0 = sb * SBK
            for i in range(2):
                S.dma("sp", lambda e, i=i: e.dma_start(out=ux[i][:, :], in_=u_x[i * 128:(i + 1) * 128, t0:t0 + W]), w=[ux[i]])
            for q in range(3):
                for j in range(NJ):
                    S.dma("sp", lambda e, q=q, j=j: e.dma_start(out=urkv[q][j][:, :], in_=u_rkv[q, j * 128:(j + 1) * 128, t0:t0 + W]), w=[urkv[q][j]])
            for i in range(2):
                shift(xs[i], ux[i], xv, i, (0, 1, 2))
            for q in range(3):
                for j in range(NJ):
                    base = q * 3
                    shift(rkv[q][j], urkv[q][j], cv, j, (base, base + 1, base + 2))
            S.op("act", lambda e: e.activation(out=xb[0][0:64, :], in_=xs[0][0:64, :], func=AF.Tanh), r=[xs[0]], w=[xb[0]])
            S.op("act", lambda e: e.copy(out=xb[0][64:128, :], in_=xs[0][64:128, :]), r=[xs[0]], w=[xb[0]])
            for j in range(NJ):
                S.op("pe", lambda e, j=j: e.matmul(ps_prep[0][:, :], lhsT=wa2_sb[:, d, j * 128:(j + 1) * 128], rhs=xb[0][:, :], start=True, stop=True), r=[wa2_sb, xb[0]], w=[ps_prep[0]])
                S.op("act", lambda e, j=j: e.activation(out=lw[j][:, :], in_=ps_prep[0][:, :], func=AF.Sigmoid, bias=cv[:, j, CV["w00"] + d:CV["w00"] + d + 1]), r=[ps_prep[0], cv], w=[lw[j]])
                S.op("pe", lambda e, j=j: e.matmul(ps_prep[1][:, :], lhsT=wa2_sb[:, 2 + d, j * 128:(j + 1) * 128], rhs=xb[0][:, :], start=True, stop=True), r=[wa2_sb, xb[0]], w=[ps_prep[1]])
                if d == 0 and sb == 0 and j == 0 and os.environ.get("DBG"):
                    S.op("dve", lambda e: e.tensor_copy(out=dbgt[:, :], in_=ps_prep[1][:, :]), r=[ps_prep[1]], w=[dbgt])
                    dbg(nc, S, "psa", dbgt, lambda: dbgt.t[:, :], [128, SBK])
                    dbg(nc, S, "xs0", xs[0], lambda: xs[0].t[:, :], [128, SBK])
                    S.op("dve", lambda e: e.tensor_copy(out=dbgt2[:, :], in_=wa2_sb.t[:, :, :].rearrange("p a b -> p (a b)")), r=[wa2_sb], w=[dbgt2])
                    dbg(nc, S, "wa2", dbgt2, lambda: dbgt2.t[:, :], [128, SBK])
                    S.op("dve", lambda e: e.tensor_copy(out=dbgt3[:, :], in_=xb[0][:, :]), r=[xb[0]], w=[dbgt3])
                    dbg(nc, S, "xb0", dbgt3, lambda: dbgt3.t[:, :], [128, SBK])
                    dbg(nc, S, "cv", cv, lambda: cv.t[:, :, :], [128, 2, NCV])
                S.op("act", lambda e, j=j: e.activation(out=a_t[d][j][:, :], in_=ps_prep[1][:, :], func=AF.Sigmoid, bias=cv[:, j, CV["a00"] + d:CV["a00"] + d + 1]), r=[ps_prep[1], cv], w=[a_t[d][j]])
                if d == 1:
                    S.op("pe", lambda e, j=j: e.matmul(ps_prep[1][:, :], lhsT=wa2_sb[:, 2, j * 128:(j + 1) * 128], rhs=xb[0][:, :], start=True, stop=True), r=[wa2_sb, xb[0]], w=[ps_prep[1]])
                    S.op("act", lambda e, j=j: e.activation(out=a_t[0][j][:, :], in_=ps_prep[1][:, :], func=AF.Sigmoid, bias=cv[:, j, CV["a00"]:CV["a00"] + 1]), r=[ps_prep[1], cv], w=[a_t[0][j]])
                S.op("dve", lambda e, j=j: e.tensor_scalar(out=lw[j][:, :], in0=lw[j][:, :], scalar1=-E05, scalar2=None, op0=ALU.mult), r=[lw[j]], w=[lw[j]])
                S.op("dve", lambda e, j=j: e.tensor_tensor_scan(out=cp[j][:, :], data0=rmask[:, :], data1=lw[j][:, :], initial=0.0, op0=ALU.mult, op1=ALU.add), r=[rmask, lw[j]], w=[cp[j]])
                if d == 1:
                    def f_suf(e, j=j):
                        v = cp[j].t[:, :].rearrange("p (c f) -> p c f", f=C)
                        last = v[:, :, C - 1:C].broadcast_to([128, NCH, C])
                        return e.tensor_tensor(out=tmp.t[:, :].rearrange("p (c f) -> p c f", f=C), in0=last, in1=v, op=ALU.subtract)
                    S.op("dve", f_suf, r=[cp[j]], w=[tmp])
                    S.op("dve", lambda e, j=j: e.tensor_tensor(out=cp[j][:, :], in0=tmp[:, :], in1=lw[j][:, :], op=ALU.add), r=[tmp, lw[j]], w=[cp[j]])
                S.op("dve", lambda e, j=j: e.tensor_tensor(out=ce[j][:, :], in0=cp[j][:, :], in1=lw[j][:, :], op=ALU.subtract), r=[cp[j], lw[j]], w=[ce[j]])
                S.op("dve", lambda e, j=j: e.tensor_scalar(out=kkt[j][:, :], in0=rkv[1][j][:, :], scalar1=cv[:, j, CV["kk"]:CV["kk"] + 1], scalar2=None, op0=ALU.mult), r=[rkv[1][j], cv], w=[kkt[j]])
                S.op("act", lambda e, j=j: e.activation(out=sqb[:, :], in_=kkt[j][:, :], func=AF.Square), r=[kkt[j]], w=[sqb])
                S.op("pe", lambda e: e.matmul(ps_prep[0][:, :], lhsT=bones[:, :], rhs=sqb[:, :], start=True, stop=True), r=[bones, sqb], w=[ps_prep[0]])
                S.op("act", lambda e: e.activation(out=tmp[:, :], in_=ps_prep[0][:, :], func=AF.Ln, bias=eps12[:, 0:1]), r=[ps_prep[0], eps12], w=[tmp])
                S.op("act", lambda e: e.activation(out=tmp[:, :], in_=tmp[:, :], func=AF.Exp, scale=-0.5), r=[tmp], w=[tmp])
                S.op("dve", lambda e, j=j: e.tensor_tensor(out=kkt[j][:, :], in0=kkt[j][:, :], in1=tmp[:, :], op=ALU.mult), r=[kkt[j], tmp], w=[kkt[j]])
                def f_kt(dd, dst):
                    S.op("dve", lambda e, j=j: e.tensor_scalar(out=tmp[:, :], in0=a_t[dd][j][:, :], scalar1=-1.0, scalar2=cv[:, j, CV["ka"]:CV["ka"] + 1], op0=ALU.add, op1=ALU.mult), r=[a_t[dd][j], cv], w=[tmp])
                    S.op("dve", lambda e, j=j: e.scalar_tensor_tensor(out=dst[:, :], in0=tmp[:, :], scalar=1.0, in1=rkv[1][j][:, :], op0=ALU.add, op1=ALU.mult), r=[tmp, rkv[1][j]], w=[dst])
                f_kt(d, kt[j])
                S.op("dve", lambda e, j=j: e.tensor_tensor(out=akk[j][:, :], in0=a_t[d][j][:, :], in1=kkt[j][:, :], op=ALU.mult), r=[a_t[d][j], kkt[j]], w=[akk[j]])
                S.op("act", lambda e, j=j: e.activation(out=ex[0][:, :], in_=ce[j][:, :], func=AF.Exp), r=[ce[j]], w=[ex[0]])
                S.op("act", lambda e, j=j: e.activation(out=ex[1][:, :], in_=cp[j][:, :], func=AF.Exp, scale=-1.0), r=[cp[j]], w=[ex[1]])
                S.op("act", lambda e, j=j: e.activation(out=ex[3][:, :], in_=cp[j][:, :], func=AF.Exp), r=[cp[j]], w=[ex[3]])
                lidx = C - 1 if d == 0 else 0
                S.op("act", lambda e, j=j, lidx=lidx: e.activation(out=PCc[j].t[:, :].rearrange("p (c o) -> p c o", o=1), in_=cp[j].t[:, :].rearrange("p (c f) -> p c f", f=C)[:, :, lidx:lidx + 1], func=AF.Exp), r=[cp[j]], w=[PCc[j]])
                for c in range(NCH):
                    li = c * C + (C - 1 if d == 0 else 0)
                    S.op("act", lambda e, j=j, c=c, li=li: e.activation(out=ex[2][:, c * C:(c + 1) * C], in_=cp[j][:, c * C:(c + 1) * C], func=AF.Exp, scale=-1.0, bias=cp[j][:, li:li + 1]), r=[cp[j]], w=[ex[2]])
                S.op("dve", lambda e, j=j: e.scalar_tensor_tensor(out=fm["At"][j][:, :], in0=kkt[j][:, :], scalar=-1.0, in1=ex[0][:, :], op0=ALU.mult, op1=ALU.mult), r=[kkt[j], ex[0]], w=[fm["At"][j]])
                S.op("pool", lambda e, j=j: e.tensor_tensor(out=fm["Bt"][j][:, :], in0=akk[j][:, :], in1=ex[1][:, :], op=ALU.mult), r=[akk[j], ex[1]], w=[fm["Bt"][j]])
                S.op("pool", lambda e, j=j: e.tensor_tensor(out=fm["Bpt"][j][:, :], in0=akk[j][:, :], in1=ex[2][:, :], op=ALU.mult), r=[akk[j], ex[2]], w=[fm["Bpt"][j]])
                S.op("dve", lambda e, j=j: e.tensor_tensor(out=fm["Kt"][j][:, :], in0=kt[j][:, :], in1=ex[1][:, :], op=ALU.mult), r=[kt[j], ex[1]], w=[fm["Kt"][j]])
                S.op("pool", lambda e, j=j: e.tensor_tensor(out=fm["Kpt"][j][:, :], in0=kt[j][:, :], in1=ex[2][:, :], op=ALU.mult), r=[kt[j], ex[2]], w=[fm["Kpt"][j]])
                S.op("pool", lambda e, j=j: e.tensor_tensor(out=fm["Rt"][j][:, :], in0=rkv[0][j][:, :], in1=ex[3][:, :], op=ALU.mult), r=[rkv[0][j], ex[3]], w=[fm["Rt"][j]])
                S.op("act", lambda e, j=j: e.copy(out=fm["Vt"][j][:, :], in_=rkv[2][j][:, :]), r=[rkv[2][j]], w=[fm["Vt"][j]])
                for n in ("At", "Bt", "Bpt", "Kt", "Kpt", "Rt", "Vt"):
                    S.op("dve", lambda e, j=j, n=n: e.tensor_copy(out=fmo[n][j][:, :], in_=fm[n][j][64:128, :]), r=[fm[n][j]], w=[fmo[n][j]])
                S.op("dve", lambda e, j=j: e.tensor_copy(out=PCo[j][:, :], in_=PCc[j][64:128, :]), r=[PCc[j]], w=[PCo[j]])
                if d == 1:
                    f_kt(0, kts[j])
                    S.op("dve", lambda e, j=j: e.tensor_tensor(out=kts[j][:, :], in0=kts[j][:, :], in1=kt[j][:, :], op=ALU.add), r=[kts[j], kt[j]], w=[kts[j]])
                    S.op("dve", lambda e, j=j: e.scalar_tensor_tensor(out=sqb[:, :], in0=rkv[0][j][:, :], scalar=cv[:, j, CV["rk"]:CV["rk"] + 1], in1=kts[j][:, :], op0=ALU.mult, op1=ALU.mult), r=[rkv[0][j], cv, kts[j]], w=[sqb])
                    S.op("pe", lambda e: e.matmul(ps_prep[0][:, :], lhsT=bones[:, :], rhs=sqb[:, :], start=True, stop=True), r=[bones, sqb], w=[ps_prep[0]])
                    S.op("dve", lambda e, j=j: e.tensor_tensor(out=bon[j][:, :], in0=ps_prep[0][:, :], in1=rkv[2][j][:, :], op=ALU.mult), r=[ps_prep[0], rkv[2][j]], w=[bon[j]])
            if d == 1:
                S.op("act", lambda e: e.activation(out=xb[1][:, :], in_=xs[1][:, :], func=AF.Sigmoid), r=[xs[1]], w=[xb[1]])
                for j in range(NJ):
                    S.op("pe", lambda e, j=j: e.matmul(ps_prep[0][:, :], lhsT=g2_sb[:, j * 128:(j + 1) * 128], rhs=xb[1][:, :], start=True, stop=True), r=[g2_sb, xb[1]], w=[ps_prep[0]])
                    S.op("act", lambda e, j=j: e.copy(out=g_t[j][:, :], in_=ps_prep[0][:, :]), r=[ps_prep[0]], w=[g_t[j]])

            if d == 0 and sb == 0:
                for n in ("At", "Bt", "Bpt", "Kt", "Kpt", "Rt", "Vt"):
                    dbg(nc, S, n, fm[n][0], lambda n=n: fm[n][0].t[:, :], [128, SBK], BF16)
                dbg(nc, S, "cp", cp[0], lambda: cp[0].t[:, :], [128, SBK])
                dbg(nc, S, "lw", lw[0], lambda: lw[0].t[:, :], [128, SBK])
                dbg(nc, S, "kk", kkt[0], lambda: kkt[0].t[:, :], [128, SBK])
                dbg(nc, S, "r", rkv[0][0], lambda: rkv[0][0].t[:, :], [128, SBK])
                dbg(nc, S, "a", a_t[0][0], lambda: a_t[0][0].t[:, :], [128, SBK])
            if os.environ.get('BARRIER'):
                S.barrier()
            if STAGE < 2:
                continue
            if os.environ.get('UNITSB') is not None and str(sb) not in os.environ['UNITSB']:
                continue
            ch_order = list(range(NCH)) if d == 0 else list(range(NCH - 1, -1, -1))
            for c in ch_order:
                cs = slice(c * C, (c + 1) * C)
                gch = sb * NCH + c
                for j in range(NJ):
                    for e in range(2):
                        p0 = 0
                        P = slice(0, 64)
                        h = j * 2 + e
                        At, Bt, Bpt, Kt, Kpt, Rt, Vt = ((fm[n][j] if e == 0 else fmo[n][j]) for n in ("At", "Bt", "Bpt", "Kt", "Kpt", "Rt", "Vt"))
                        PCh = PCc[j] if e == 0 else PCo[j]
                        fmr = [At, Bt, Bpt, Kt, Kpt, Rt, Vt]
                        S.op("pe", lambda e_, P=P, e=e: e_.matmul(pa(PS1[e], lo=0, hi=128), lhsT=Bt.t[P, cs], rhs=At.t[P, cs], start=True, stop=True), r=[Bt, At], w=[PS1[e][3]])
                        S.op("pe", lambda e_, P=P, e=e: e_.matmul(pa(PS1[e], lo=128, hi=256), lhsT=Bt.t[P, cs], rhs=Rt.t[P, cs], start=True, stop=True), r=[Bt, Rt], w=[PS1[e][3]])
                        S.op("pe", lambda e_, P=P, e=e: e_.matmul(pa(PS2[e], lo=0, hi=128), lhsT=Kt.t[P, cs], rhs=At.t[P, cs], start=True, stop=True), r=[Kt, At], w=[PS2[e][3]])
                        S.op("pe", lambda e_, P=P, e=e: e_.matmul(pa(PS2[e], lo=128, hi=256), lhsT=Kt.t[P, cs], rhs=Rt.t[P, cs], start=True, stop=True), r=[Kt, Rt], w=[PS2[e][3]])
                        S.op("pe", lambda e_, P=P, e=e: e_.matmul(pa(PS3[e]), lhsT=At.t[P, cs], rhs=Bt.t[P, cs], start=True, stop=True), r=[At, Bt], w=[PS3[e][3]])
                        ps4 = PS4[e][0].t[:, PS4[e][1]:PS4[e][2]].bitcast(BF16)
                        for qi, src in enumerate((At, Bpt, Kpt, Vt)):
                            S.op("pe", lambda e_, P=P, qi=qi, src=src, ps4=ps4: e_.transpose(out=ps4[:, qi * 64:(qi + 1) * 64], in_=src.t[P, cs], identity=ident.t[P, p0:p0 + 64]), r=[src, ident], w=[PS4[e][3]])
                        if STAGE < 3:
                            continue
                        if 'a' in SUB: S.op("dve", lambda e_, e=e: e_.tensor_tensor(out=X1[e][:, :], in0=pa(PS1[e]), in1=mk2[d][:, :], op=ALU.mult), r=[PS1[e][3], mk2[d]], w=[X1[e]])
                        if 'a' in SUB: S.op("dve", lambda e_, e=e: e_.tensor_tensor(out=X2[e][:, :], in0=pa(PS2[e]), in1=mk2[d][:, :], op=ALU.mult), r=[PS2[e][3], mk2[d]], w=[X2[e]])
                        if 'a' in SUB: S.op("dve", lambda e_, e=e: e_.tensor_tensor(out=Lm[e][0][:, :], in0=pa(PS3[e]), in1=mkL[d][:, :], op=ALU.mult), r=[PS3[e][3], mkL[d]], w=[Lm[e][0]])
                        if 'b' in SUB: S.op("dve", lambda e_, e=e, ps4=ps4: e_.tensor_copy(out=TM[e][:, :], in_=ps4), r=[PS4[e][3]], w=[TM[e]])
                        if d == 0 and sb == 0 and c == 0 and j == 0 and e == 0:
                            dbg(nc, S, "X1", X1[0], lambda: X1[0].t[:, :], [128, 256], BF16)
                            dbg(nc, S, "X2", X2[0], lambda: X2[0].t[:, :], [128, 256], BF16)
                            dbg(nc, S, "L0", Lm[0][0], lambda: Lm[0][0].t[:, :], [128, 128], BF16)
                            dbg(nc, S, "TM", TM[0], lambda: TM[0].t[:, :], [128, 256], BF16)
                        if STAGE < 4:
                            continue
                        S.op("pe", lambda e_, e=e: e_.matmul(pa(PS5), lhsT=X2[e][:, 0:128], rhs=TM[e][:, 192:256], start=True, stop=True), r=[X2[e], TM[e]], w=[PS5[3]])
                        S.op("act", lambda e_, e=e: e_.copy(out=Z[e][0][:, 0:64], in_=TM[e][:, 0:64]), r=[TM[e]], w=[Z[e][0]])
                        S.op("dve", lambda e_, e=e: e_.tensor_copy(out=Z[e][0][:, 64:128], in_=pa(PS5)), r=[PS5[3]], w=[Z[e][0]])
                        Lsrc = Lm[e][0]
                        for lv in range(7):
                            S.op("pool", lambda e_, lv=lv, e=e, Lsrc=Lsrc: e_.tensor_tensor(out=Lmk[e][lv][:, :], in0=Lsrc.t[:, :], in1=lvm[:, lv, d, :], op=ALU.mult), r=[Lsrc, lvm], w=[Lmk[e][lv]])
                        Tb = ident
                        TbT = ident
                        for lv in range(7):
                            S.op("pe", lambda e_, lv=lv, e=e, TbT=TbT: e_.matmul(pa(PSn), lhsT=Lmk[e][lv][:, :], rhs=TbT.t[:, :], start=True, stop=True), r=[Lmk[e][lv], TbT], w=[PSn[3]])
                            S.op("act", lambda e_, e=e: e_.copy(out=XT[e][:, :], in_=pa(PSn)), r=[PSn[3]], w=[XT[e]])
                            S.op("pe", lambda e_, e=e, Tb=Tb: e_.matmul(pa(PSz), lhsT=XT[e][:, :], rhs=Tb.t[:, :], start=True, stop=True), r=[XT[e], Tb], w=[PSz[3]])
                            S.op("pe", lambda e_, e=e, Tb=Tb: e_.matmul(pa(PSl), lhsT=Tb.t[:, :], rhs=XT[e][:, :], start=True, stop=True), r=[XT[e], Tb], w=[PSl[3]])
                            Tn = Tbuf[e][lv % 2]
                            TnT = TTbuf[e][lv % 2]
                            S.op("dve", lambda e_, Tn=Tn, Tb=Tb: e_.tensor_tensor(out=Tn[:, :], in0=pa(PSz), in1=Tb.t[:, :], op=ALU.add), r=[PSz[3], Tb], w=[Tn])
                            S.op("dve", lambda e_, TnT=TnT, TbT=TbT: e_.tensor_tensor(out=TnT[:, :], in0=pa(PSl), in1=TbT.t[:, :], op=ALU.add), r=[PSl[3], TbT], w=[TnT])
                            Tb, TbT = Tn, TnT
                        S.op("pe", lambda e_, e=e, TbT=TbT: e_.matmul(pa(PSz), lhsT=TbT.t[:, :], rhs=Z[e][0][:, :], start=True, stop=True), r=[TbT, Z[e][0]], w=[PSz[3]])
                        S.op("dve", lambda e_, e=e: e_.tensor_copy(out=Z[e][1][:, :], in_=pa(PSz)), r=[PSz[3]], w=[Z[e][1]])
                        zc = 1
                        Zf = Z[e][zc]
                        if d == 0 and sb == 0 and c == 0 and j == 0 and e == 0:
                            dbg(nc, S, "Zf", Zf, lambda Zf=Zf: Zf.t[:, :], [128, 128], BF16)
                        if STAGE < 5:
                            continue
                        if 'q' in SUB5: S.op("pe", lambda e_, Zf=Zf, e=e, P=P: e_.matmul(pa(PSq), lhsT=Zf.t[:, 0:128], rhs=X1[e][:, 128:256], start=True, stop=True), r=[Zf, X1[e]], w=[PSq[3]])
                        if 'q' in SUB5: S.op("dve", lambda e_, e=e, P=P: e_.tensor_tensor(out=Qt[e][P, :], in0=pa(PSq, p0, p0 + 64), in1=Rt.t[P, cs], op=ALU.add), r=[PSq[3], Rt], w=[Qt[e]])
                        li = c * C + (C - 1 if d == 0 else 0)
                        if 'g' in SUB5: S.op("pe", lambda e_, Zf=Zf, e=e: e_.matmul(pa(PSg), lhsT=Zf.t[:, 0:128], rhs=TM[e][:, 64:128], start=True, stop=True), r=[Zf, TM[e]], w=[PSg[3]])
                        if 'g' in SUB5: S.op("dve", lambda e_, e=e, P=P, li=li: e_.scalar_tensor_tensor(out=Gt[e][P, 0:64], in0=identf.t[P, p0:p0 + 64], scalar=PCh.t[P, c:c + 1], in1=pa(PSg, p0, p0 + 64), op0=ALU.mult, op1=ALU.add), r=[identf, PCh, PSg[3]], w=[Gt[e]])
                        if 'h' in SUB5: S.op("pe", lambda e_, Zf=Zf, e=e: e_.matmul(pa(PSh), lhsT=TM[e][:, 64:192], rhs=Zf.t[:, 64:128], start=True, stop=False), r=[Zf, TM[e]], w=[PSh[3]])
                        if 'h' in SUB5: S.op("pe", lambda e_, e=e: e_.matmul(pa(PSh), lhsT=TM[e][:, 128:256], rhs=TM[e][:, 192:256], start=False, stop=True), r=[TM[e]], w=[PSh[3]])
                        if 'h' in SUB5: S.op("act", lambda e_, e=e, P=P: e_.copy(out=Hs[e][P, :], in_=pa(PSh, p0, p0 + 64)), r=[PSh[3]], w=[Hs[e]])
                        if d == 0 and sb == 0 and c == 0 and j == 0 and e == 0:
                            dbg(nc, S, "Qt", Qt[0], lambda: Qt[0].t[0:64, :], [64, 128], BF16)
                            dbg(nc, S, "Gt", Gt[0], lambda: Gt[0].t[0:64, :], [64, 64], BF16)
                            dbg(nc, S, "Hs", Hs[0], lambda: Hs[0].t[0:64, :], [64, 64])
                        if STAGE < 6:
                            continue
                        ci = cur[j][e]
                        Scur = Sth[j][ci][e]
                        S.op("pe", lambda e_, Zf=Zf, e=e: e_.matmul(pa(PSY[e]), lhsT=X1[e][:, 128:256], rhs=Zf.t[:, 64:128], start=True, stop=False), r=[X1[e], Zf], w=[PSY[e][3]])
                        S.op("pe", lambda e_, e=e: e_.matmul(pa(PSY[e]), lhsT=X2[e][:, 128:256], rhs=TM[e][:, 192:256], start=False, stop=False), r=[X2[e], TM[e]], w=[PSY[e][3]])
                        if 'b' in SUB6: S.op("pe", lambda e_, e=e, P=P, Scur=Scur: e_.matmul(pa(PSY[e]), lhsT=Qt[e][:, :], rhs=Scur.t[:, :], start=False, stop=True), r=[Qt[e], Scur], w=[PSY[e][3]])
                        S.op("dve", lambda e_, e=e, h=h: e_.tensor_copy(out=ytok[:, h * 64:(h + 1) * 64], in_=pa(PSY[e])), r=[PSY[e][3]], w=[ytok_reg[h]])
                        if STAGE < 7:
                            continue
                        Snew = Sth[j][1 - ci][e]
                        S.op("pe", lambda e_, e=e, P=P, Scur=Scur: e_.matmul(pa(PSs[e]), lhsT=Gt[e][:, :], rhs=Scur.t[:, :], start=True, stop=True), r=[Gt[e], Scur], w=[PSs[e][3]])
                        S.op("dve", lambda e_, e=e, P=P, Snew=Snew: e_.tensor_tensor(out=Snew.t[P, :], in0=pa(PSs[e], p0, p0 + 64), in1=Hs[e][P, :], op=ALU.add), r=[PSs[e][3], Hs[e]], w=[Snew])
                        cur[j][e] = 1 - ci
                        if os.environ.get('UBAR'):
                            S.barrier()
                if d == 0 and sb == 0 and c == 0:
                    dbg(nc, S, "ytok", ytok_reg, lambda: ytok.t[:, :], [128, 256])
                if STAGE < 8:
                    continue
                if d == 0:
                    S.dma("sp", lambda e_, gch=gch: e_.dma_start(out=yacc_d[gch, :, :], in_=ytok[:, :]), r=ytok_reg, w=[yacc_reg[gch]], final=(1 not in dirs))
                else:
                    S.dma("sp", lambda e_, gch=gch: e_.dma_start(out=yprev[:, :], in_=yacc_d[gch, :, :]), r=[yacc_reg[gch]], w=[yprev])
                    S.op("dve", lambda e_: e_.tensor_tensor(out=yn[:, :], in0=ytok[:, :], in1=yprev[:, :], op=ALU.add), r=ytok_reg + [yprev], w=[yn])
                    for h in range(NH):
                        S.op("dve", lambda e_, h=h: e_.bn_stats(out=stats[:, h, :], in_=yn[:, h * 64:(h + 1) * 64]), r=[yn], w=[stats])
                    for h in range(NH):
                        S.op("dve", lambda e_, h=h: e_.bn_aggr(out=mv[:, h, :], in_=stats[:, h, :]), r=[stats], w=[mv])
                    S.op("act", lambda e_: e_.activation(out=rstd[:, :], in_=mv[:, :, 1], func=AF.Ln, bias=epsgn[:, 0:1]), r=[mv, epsgn], w=[rstd])
                    S.op("act", lambda e_: e_.activation(out=rstd[:, :], in_=rstd[:, :], func=AF.Exp, scale=-0.5), r=[rstd], w=[rstd])
                    for h in range(NH):
                        S.op("dve", lambda e_, h=h: e_.tensor_scalar(out=yn[:, h * 64:(h + 1) * 64], in0=yn[:, h * 64:(h + 1) * 64], scalar1=mv[:, h, 0:1], scalar2=rstd[:, h:h + 1], op0=ALU.subtract, op1=ALU.mult), r=[yn, mv, rstd], w=[yn])
                    S.op("dve", lambda e_: e_.tensor_tensor(out=yn[:, :], in0=yn[:, :], in1=gnwb_sb[:, 0, :], op=ALU.mult), r=[yn, gnwb_sb], w=[yn])
                    S.op("dve", lambda e_: e_.tensor_tensor(out=yn[:, :], in0=yn[:, :], in1=gnwb_sb[:, 1, :], op=ALU.add), r=[yn, gnwb_sb], w=[yn])
                    for j in range(NJ):
                        S.op("pe", lambda e_, j=j: e_.transpose(out=pa(PStr), in_=yn[:, j * 128:(j + 1) * 128], identity=identf[:, :]), r=[yn, identf], w=[PStr[3]])
                        S.op("dve", lambda e_, j=j: e_.tensor_tensor(out=yo[j][:, cs], in0=pa(PStr), in1=bon[j][:, cs], op=ALU.add), r=[PStr[3], bon[j]], w=[yo[j]])
                        S.op("dve", lambda e_, j=j: e_.tensor_tensor(out=yo[j][:, cs], in0=yo[j][:, cs], in1=g_t[j][:, cs], op=ALU.mult), r=[yo[j], g_t[j]], w=[yo[j]])
            if d == 1:
                for j in range(NJ):
                    S.dma("sp", lambda e_, j=j: e_.dma_start(out=d_out[j * 128:(j + 1) * 128, t0:t0 + SBK], in_=yo[j][:, :]), r=[yo[j]], final=True)


TQ = 2048
TK = 4096
HALO = 1024
BR = (1, 4, 16)
A_NORM_EPS = 1e-6


def key_tiles():
    out = []
    for d in BR:
        nb = TQ // (128 * d)
        for r in range(d):
            for i in range(nb + 1):
                out.append((d, r, i, HALO // d - 64 + 128 * i))
    return out


KT = key_tiles()
NKT = len(KT)


def emit_attn(nc, S, d_in, d_out, consts):
    qT = d_in["qT"]
    kT = d_in["kT"]
    vT = d_in["vT"]
    vld = d_in["vld"]
    qkg = d_in["qkg"]
    ident = consts["ident"]

    ones_f = S.sb([128, 512], F32, "a_ones")
    S.op("pool", lambda e: e.memset(ones_f[:, :], 1.0), w=[ones_f])
    bones = S.sb([128, 128], BF16, "a_bones")
    S.op("pool", lambda e: e.memset(bones[:, :], 0.0), w=[bones])
    S.op("pool", lambda e: e.memset(bones[0:64, 0:64], 1.0), w=[bones])
    S.op("pool", lambda e: e.memset(bones[64:128, 64:128], 1.0), w=[bones])
    vld_sb = S.sb([128, NKT], F32, "vld_sb")
    S.dma("sp", lambda e: e.dma_start(out=vld_sb[:, :], in_=vld[:, :]), w=[vld_sb])
    qkg_sb = S.sb([128, 2], F32, "qkg_sb")
    S.dma("sp", lambda e: e.dma_start(out=qkg_sb[:, :], in_=qkg[:, :]), w=[qkg_sb])
    qsc = S.sb([128, 1], F32, "qsc")
    S.op("dve", lambda e: e.tensor_scalar(out=qsc[:, :], in0=qkg_sb[:, 0:1], scalar1=0.125, scalar2=None, op0=ALU.mult), r=[qkg_sb], w=[qsc])
    epsb = S.sb([128, 1], F32, "a_eps")
    S.op("pool", lambda e: e.memset(epsb[:, :], A_NORM_EPS), w=[epsb])

    reli = S.sb([128, 128], I32, "reli")
    relf = S.sb([128, 128], F32, "relf")
    band = S.sb([128, 128], F32, "band")
    mtmp = S.sb([128, 128], F32, "mtmp")
    masks = {}
    for d in BR:
        for ab, base in (("A", -64), ("B", 64)):
            m = S.sb([128, 8, 128], BF16, "mask_%d%s" % (d, ab))
            S.op("pool", lambda e, base=base: e.iota(reli[:, :], pattern=[[-1, 128]], base=base, channel_multiplier=1), w=[reli])
            S.op("dve", lambda e: e.tensor_copy(out=relf[:, :], in_=reli[:, :]), r=[reli], w=[relf])
            S.op("act", lambda e: e.activation(out=relf[:, :], in_=relf[:, :], func=AF.Abs), r=[relf], w=[relf])
            S.op("dve", lambda e: e.tensor_single_scalar(out=band[:, :], in_=relf[:, :], scalar=64.5, op=ALU.is_le), r=[relf], w=[band])
            for h in range(8):
                slope = 2.0 ** (-(h + 1))
                S.op("act", lambda e, slope=slope, d=d: e.activation(out=mtmp[:, :], in_=relf[:, :], func=AF.Exp, scale=-slope * d), r=[relf], w=[mtmp])
                S.op("dve", lambda e, m=m, h=h: e.tensor_tensor(out=m[:, h, :], in0=mtmp[:, :], in1=band[:, :], op=ALU.mult), r=[mtmp, band], w=[m])
            masks[(d, ab)] = m

    stage = S.sb([128, TK], F32, "stage")
    sqb = S.sb([128, 512], BF16, "a_sqb")
    rs = S.sb([128, 512], F32, "a_rs")
    Qb = [S.sb([128, 2, TQ], BF16, "Qb%d" % i) for i in range(2)]
    Kb = [S.sb([128, TK], BF16, "Kb%d" % i) for i in range(2)]
    Vb = [S.sb([128, TK], BF16, "Vb%d" % i) for i in range(2)]
    acc = S.sb([128, 4, TQ], F32, "acc")
    acc_reg = [[Reg("acc%d_%d" % (h, qb)) for qb in range(TQ // 128)] for h in range(4)]
    NVX = 17
    Vx = [S.sb([128, 4, 128], BF16, "Vx%d" % i) for i in range(NVX)]
    ex = [S.sb([128, 256], F32, "a_ex%d" % i) for i in range(2)]
    Pt = [S.sb([128, 256], BF16, "a_Pt%d" % i) for i in range(4)]
    rec = S.sb([64, TQ], F32, "rec")
    o_sb = S.sb([64, TQ], F32, "o_sb")
    ps_n = S.ps([128, 512], F32, "a_ps_n")
    ps_s = [S.ps([128, 512], F32, "a_ps_s%d" % i) for i in range(2)]
    ps_t = S.ps([128, 512], F32, "a_ps_t")
    ps_o = [S.ps([128, 512], F32, "a_ps_o%d" % i) for i in range(2)]
    po_reg = [[Reg("po%d_%d" % (i, k)) for k in range(2)] for i in range(2)]
    for i in range(2):
        S.op("pool", lambda e, i=i: e.memset(Qb[i][:, :, :], 0.0), w=[Qb[i]])

    def qknorm(src_rows, ncols, gcol, dst_fn):
        S.dma("sp", lambda e: e.dma_start(out=stage[:, 0:ncols], in_=src_rows), w=[stage])
        for c0 in range(0, ncols, 512):
            S.op("act", lambda e, c0=c0: e.activation(out=sqb[:, :], in_=stage[:, c0:c0 + 512], func=AF.Square), r=[stage], w=[sqb])
            S.op("pe", lambda e: e.matmul(ps_n[:, :], lhsT=bones[:, :], rhs=sqb[:, :], start=True, stop=True), r=[bones, sqb], w=[ps_n])
            S.op("act", lambda e: e.activation(out=rs[:, :], in_=ps_n[:, :], func=AF.Ln, scale=1.0 / 64, bias=epsb[:, 0:1]), r=[ps_n, epsb], w=[rs])
            S.op("act", lambda e: e.activation(out=rs[:, :], in_=rs[:, :], func=AF.Exp, scale=-0.5), r=[rs], w=[rs])
            for out_ap, ps in dst_fn(c0):
                S.op("dve", lambda e, out_ap=out_ap, ps=ps, c0=c0: e.scalar_tensor_tensor(out=out_ap, in0=stage[ps, c0:c0 + 512], scalar=gcol[ps, 0:1], in1=rs[ps, :], op0=ALU.mult, op1=ALU.mult), r=[stage, rs, qsc, qkg_sb], w=[dst_fn.tile])

    for grp in range(2):
        for pi in range(2):
            pair = grp * 2 + pi
            rows = slice(pair * 128, (pair + 1) * 128)

            def dq(c0, pi=pi):
                return [(Qb[pi].t[0:64, 0, c0:c0 + 512], slice(0, 64)), (Qb[pi].t[64:128, 1, c0:c0 + 512], slice(64, 128))]
            dq.tile = Qb[pi]
            qknorm(qT[rows, :], TQ, qsc, dq)

            def dk(c0, pi=pi):
                return [(Kb[pi].t[:, c0:c0 + 512], slice(0, 128))]
            dk.tile = Kb[pi]
            qknorm(kT[rows, :], TK, qkg_sb.t[:, 1:2], dk)
            S.dma("sp", lambda e, rows=rows: e.dma_start(out=stage[:, :], in_=vT[rows, :]), w=[stage])
            S.op("act", lambda e, pi=pi: e.copy(out=Vb[pi][:, :], in_=stage[:, :]), r=[stage], w=[Vb[pi]])
        kt_idx = 0
        first_branch = True
        pti = 0
        for d in BR:
            nb = TQ // (128 * d)
            for r in range(d):
                for i in range(nb + 1):
                    m0 = HALO // d - 64 + 128 * i
                    p0 = r + d * m0
                    psl = slice(p0, p0 + 127 * d + 1, d)
                    pst = ps_t.t[:, 0:128].bitcast(BF16)
                    for pi in range(2):
                        S.op("pe", lambda e, pi=pi, psl=psl, pst=pst: e.transpose(out=pst[:, pi * 128:(pi + 1) * 128], in_=Vb[pi].t[:, psl], identity=ident[:, :]), r=[Vb[pi], ident], w=[ps_t])
                    S.op("dve", lambda e, i=i, pst=pst: e.tensor_copy(out=Vx[i].t[:, :, 0:64], in_=pst.rearrange("p (h x) -> p h x", x=64)), r=[ps_t], w=[Vx[i]])
                    S.op("pool", lambda e, i=i, k=kt_idx + i: e.tensor_scalar(out=Vx[i].t[:, :, 64:128], in0=ones_f.t[:, 0:256].rearrange("p (h x) -> p h x", x=64), scalar1=vld_sb[:, k:k + 1], scalar2=None, op0=ALU.mult), r=[ones_f, vld_sb], w=[Vx[i]])
                for J in range(nb):
                    mq0 = HALO // d + 128 * J
                    q0 = r + d * mq0 - HALO
                    qsl = slice(q0, q0 + 127 * d + 1, d)
                    qb0 = q0 // 128
                    pts = []
                    for ti, ab in ((J, "A"), (J + 1, "B")):
                        m0 = HALO // d - 64 + 128 * ti
                        p0 = r + d * m0
                        psl = slice(p0, p0 + 127 * d + 1, d)
                        for pi in range(2):
                            pss = ps_s[pi]
                            S.op("pe", lambda e, pi=pi, psl=psl, qsl=qsl, pss=pss: e.matmul(pss.t[:, 0:256].rearrange("p (a b) -> p a b", a=2), lhsT=Kb[pi].t[:, psl], rhs=Qb[pi].t[:, :, qsl], start=True, stop=True), r=[Kb[pi], Qb[pi]], w=[pss])
                            exi = ex[pi]
                            S.op("act", lambda e, pss=pss, exi=exi: e.activation(out=exi[:, :], in_=pss.t[:, 0:256], func=AF.Exp), r=[pss], w=[exi])
                            pt = Pt[pti % 4]
                            pti += 1
                            pair = grp * 2 + pi
                            S.op("dve", lambda e, pt=pt, exi=exi, d=d, ab=ab, pair=pair: e.tensor_tensor(out=pt.t[:, :].rearrange("p (a b) -> p a b", a=2), in0=exi.t[:, :].rearrange("p (a b) -> p a b", a=2), in1=masks[(d, ab)].t[:, pair * 2:pair * 2 + 2, :], op=ALU.mult), r=[exi, masks[(d, ab)]], w=[pt])
                            pts.append((ti, pi, pt))
                    for hh in range(4):
                        pi, hp = hh // 2, hh % 2
                        po = ps_o[hh // 2]
                        col = (hh % 2) * 128
                        mm = [(ti, pt) for (ti, ppi, pt) in pts if ppi == pi]
                        for n_, (ti, pt) in enumerate(mm):
                            S.op("pe", lambda e, ti=ti, pt=pt, hh=hh, hp=hp, po=po, col=col, n_=n_: e.matmul(po.t[:, col:col + 128], lhsT=Vx[ti].t[:, hh, :], rhs=pt.t[:, hp * 128:(hp + 1) * 128], start=(n_ == 0), stop=(n_ == len(mm) - 1)), r=[Vx[ti], pt], w=[po_reg[hh // 2][hh % 2]])
                        areg = acc_reg[hh][qb0] if d == 1 else [acc_reg[hh][q] for q in range(TQ // 128)]
                        if first_branch:
                            S.op("dve", lambda e, hh=hh, po=po, col=col, qsl=qsl: e.tensor_copy(out=acc.t[:, hh, qsl], in_=po.t[:, col:col + 128]), r=[po_reg[hh // 2][hh % 2]], w=[areg])
                        else:
                            S.op("dve", lambda e, hh=hh, po=po, col=col, qsl=qsl: e.tensor_tensor(out=acc.t[:, hh, qsl], in0=po.t[:, col:col + 128], in1=acc.t[:, hh, qsl], op=ALU.add), r=[po_reg[hh // 2][hh % 2], areg], w=[areg])
                kt_idx += nb + 1
            first_branch = False
        allreg = lambda hh: [acc_reg[hh][q] for q in range(TQ // 128)]
        for hh in range(4):
            S.op("dve", lambda e, hh=hh: e.reciprocal(out=rec[:, :], in_=acc.t[64:128, hh, :]), r=allreg(hh), w=[rec])
            S.op("dve", lambda e, hh=hh: e.tensor_tensor(out=o_sb[:, :], in0=acc.t[0:64, hh, :], in1=rec[:, :], op=ALU.mult), r=allreg(hh) + [rec], w=[o_sb])
            h = grp * 4 + hh
            S.dma("sp", lambda e, h=h: e.dma_start(out=d_out[h * 64:(h + 1) * 64, :], in_=o_sb[:, :]), r=[o_sb], final=True)


D = 1024
DFF = 4096
NPROJ = 3328
TOK = 2048
TB = 512
NBLK = TOK // TB
T_NORM_EPS = 1e-6


def emit_tok(nc, S, d_in, d_out, do_mix, do_proj):
    xT_d = d_in["xT"]
    x = S.sb([128, 8, TOK], F32, "x_sb")
    xreg = [Reg("x_blk%d" % b) for b in range(NBLK)]
    hb = S.sb([128, 8, TOK], BF16, "hb")
    hreg = [Reg("hb%d" % b) for b in range(NBLK)]
    ones = S.sb([128, 128], BF16, "t_ones")
    epsb = S.sb([128, 1], F32, "t_eps")
    sq = S.sb([128, 8, TB], BF16, "t_sq")
    rstd = S.sb([128, TB], F32, "t_rstd")
    gv = S.sb([128, 2, 8], F32, "gv_sb")
    wA = [S.sb([128, 8, 512], BF16, "wA%d" % i) for i in range(2)]
    wB = [S.sb([128, 4, 1024], BF16, "wB%d" % i) for i in range(2)]
    hid = [S.sb([128, 4, TB], BF16, "hid%d" % i) for i in range(2)]
    rl = [S.sb([128, TB], F32, "t_rl%d" % i) for i in range(2)]
    stg = [S.sb([128, TB], F32, "stg%d" % i) for i in range(2)]
    ps_ss = S.ps([128, TB], F32, "t_ps_ss")
    ps_a = [S.ps([128, TB], F32, "t_ps_a%d" % i) for i in range(3)]
    ps_b = [S.ps([128, TB], F32, "t_ps_b%d" % i) for i in range(3)]

    for b in range(NBLK):
        S.dma("sp", lambda e, b=b: e.dma_start(out=x[:, :, b * TB:(b + 1) * TB], in_=xT_d.rearrange("(c p) t -> p c t", p=128)[:, :, b * TB:(b + 1) * TB]), w=[xreg[b]])
    S.dma("sp", lambda e: e.dma_start(out=gv[:, :, :], in_=d_in["gv"][:, :, :]), w=[gv])
    S.op("pool", lambda e: e.memset(ones[:, :], 1.0), w=[ones])
    S.op("pool", lambda e: e.memset(epsb[:, :], T_NORM_EPS), w=[epsb])
    cnt = {"a": 0, "b": 0, "pa": 0, "pb": 0, "rl": 0, "hid": 0, "stg": 0}

    def rmsnorm(which):
        for b in range(NBLK):
            ts = slice(b * TB, (b + 1) * TB)
            for c in range(8):
                S.op("act", lambda e, c=c: e.activation(out=sq[:, c, :], in_=x[:, c, ts], func=AF.Square), r=[xreg[b]], w=[sq])
            for c in range(8):
                S.op("pe", lambda e, c=c: e.matmul(ps_ss[:, :], lhsT=ones[:, :], rhs=sq[:, c, :], start=(c == 0), stop=(c == 7)), r=[ones, sq], w=[ps_ss])
            S.op("act", lambda e: e.activation(out=rstd[:, :], in_=ps_ss[:, :], func=AF.Ln, scale=1.0 / D, bias=epsb[:, 0:1]), r=[ps_ss, epsb], w=[rstd])
            S.op("act", lambda e: e.activation(out=rstd[:, :], in_=rstd[:, :], func=AF.Exp, scale=-0.5), r=[rstd], w=[rstd])
            for c in range(8):
                S.op("dve", lambda e, c=c: e.scalar_tensor_tensor(out=hb[:, c, ts], in0=x[:, c, ts], scalar=gv[:, which, c:c + 1], in1=rstd[:, :], op0=ALU.mult, op1=ALU.mult), r=[xreg[b], gv, rstd], w=[hreg[b]])

    if do_mix:
        catT = d_in["catT"]
        for b in range(NBLK):
            ts = slice(b * TB, (b + 1) * TB)
            for c in range(8):
                st = stg[cnt["stg"] % 2]
                cnt["stg"] += 1
                S.dma("sp", lambda e, st=st, c=c: e.dma_start(out=st[:, :], in_=catT[c * 128:(c + 1) * 128, ts]), w=[st])
                S.op("act", lambda e, st=st, c=c: e.copy(out=hb[:, c, ts], in_=st[:, :]), r=[st], w=[hreg[b]])
        w_out = d_in["w_out"]
        for g in range(2):
            wt = wA[cnt["a"] % 2]
            cnt["a"] += 1
            S.dma("pool", lambda e, wt=wt, g=g: e.dma_start(out=wt[:, :, :], in_=w_out.rearrange("(c p) n -> p c n", p=128)[:, :, g * 512:(g + 1) * 512]), w=[wt])
            for b in range(NBLK):
                ts = slice(b * TB, (b + 1) * TB)
                for mm in range(4):
                    m = g * 4 + mm
                    pd = ps_a[cnt["pa"] % 3]
                    cnt["pa"] += 1
                    for c in range(8):
                        S.op("pe", lambda e, c=c, mm=mm, wt=wt, pd=pd: e.matmul(pd[:, :], lhsT=wt[:, c, mm * 128:(mm + 1) * 128], rhs=hb[:, c, ts], start=(c == 0), stop=(c == 7)), r=[wt, hreg[b]], w=[pd])
                    S.op("dve", lambda e, m=m, pd=pd: e.tensor_tensor(out=x[:, m, ts], in0=pd[:, :], in1=x[:, m, ts], op=ALU.add), r=[pd, xreg[b]], w=[xreg[b]])
        rmsnorm(0)
        w_up = d_in["w_up"]
        w_down = d_in["w_down"]
        for fg in range(8):
            wu = wA[cnt["a"] % 2]
            cnt["a"] += 1
            wd = wB[cnt["b"] % 2]
            cnt["b"] += 1
            S.dma("pool", lambda e, wu=wu, fg=fg: e.dma_start(out=wu[:, :, :], in_=w_up.rearrange("(c p) n -> p c n", p=128)[:, :, fg * 512:(fg + 1) * 512]), w=[wu])
            S.dma("pool", lambda e, wd=wd, fg=fg: e.dma_start(out=wd[:, :, :], in_=w_down[fg * 512:(fg + 1) * 512, :].rearrange("(n p) m -> p n m", p=128)), w=[wd])
            for b in range(NBLK):
                ts = slice(b * TB, (b + 1) * TB)
                hd = hid[cnt["hid"] % 2]
                cnt["hid"] += 1
                for nn in range(4):
                    pu = ps_a[cnt["pa"] % 3]
                    cnt["pa"] += 1
                    for c in range(8):
                        S.op("pe", lambda e, c=c, nn=nn, wu=wu, pu=pu: e.matmul(pu[:, :], lhsT=wu[:, c, nn * 128:(nn + 1) * 128], rhs=hb[:, c, ts], start=(c == 0), stop=(c == 7)), r=[wu, hreg[b]], w=[pu])
                    r_ = rl[cnt["rl"] % 2]
                    cnt["rl"] += 1
                    S.op("act", lambda e, pu=pu, r_=r_: e.activation(out=r_[:, :], in_=pu[:, :], func=AF.Relu), r=[pu], w=[r_])
                    S.op("pool", lambda e, nn=nn, r_=r_, hd=hd: e.tensor_tensor(out=hd[:, nn, :], in0=r_[:, :], in1=r_[:, :], op=ALU.mult), r=[r_], w=[hd])
                for m in range(8):
                    pd = ps_b[cnt["pb"] % 3]
                    cnt["pb"] += 1
                    for nn in range(4):
                        S.op("pe", lambda e, nn=nn, m=m, wd=wd, pd=pd, hd=hd: e.matmul(pd[:, :], lhsT=wd[:, nn, m * 128:(m + 1) * 128], rhs=hd[:, nn, :], start=(nn == 0), stop=(nn == 3)), r=[wd, hd], w=[pd])
                    S.op("dve", lambda e, m=m, pd=pd: e.tensor_tensor(out=x[:, m, ts], in0=pd[:, :], in1=x[:, m, ts], op=ALU.add), r=[pd, xreg[b]], w=[xreg[b]])
        xo = d_out["xT"]
        for b in range(NBLK):
            S.dma("sp", lambda e, b=b: e.dma_start(out=xo.rearrange("(c p) t -> p c t", p=128)[:, :, b * TB:(b + 1) * TB], in_=x[:, :, b * TB:(b + 1) * TB]), r=[xreg[b]], final=True)
    if do_proj:
        rmsnorm(1)
        w_in = d_in["w_in"]
        po = d_out["projT"]
        ngrp = (NPROJ + 511) // 512
        for g in range(ngrp):
            ncol = min(512, NPROJ - g * 512)
            wt = wA[cnt["a"] % 2]
            cnt["a"] += 1
            S.dma("pool", lambda e, wt=wt, g=g, ncol=ncol: e.dma_start(out=wt[:, :, 0:ncol], in_=w_in.rearrange("(c p) n -> p c n", p=128)[:, :, g * 512:g * 512 + ncol]), w=[wt])
            for b in range(NBLK):
                ts = slice(b * TB, (b + 1) * TB)
                for mm in range(ncol // 128):
                    n = g * 4 + mm
                    pd = ps_a[cnt["pa"] % 3]
                    cnt["pa"] += 1
                    for c in range(8):
                        S.op("pe", lambda e, c=c, mm=mm, wt=wt, pd=pd: e.matmul(pd[:, :], lhsT=wt[:, c, mm * 128:(mm + 1) * 128], rhs=hb[:, c, ts], start=(c == 0), stop=(c == 7)), r=[wt, hreg[b]], w=[pd])
                    st = stg[cnt["stg"] % 2]
                    cnt["stg"] += 1
                    if cnt["stg"] % 2:
                        S.op("act", lambda e, st=st, pd=pd: e.copy(out=st[:, :], in_=pd[:, :]), r=[pd], w=[st])
                    else:
                        S.op("dve", lambda e, st=st, pd=pd: e.tensor_copy(out=st[:, :], in_=pd[:, :]), r=[pd], w=[st])
                    S.dma("sp", lambda e, st=st, n=n: e.dma_start(out=po[n * 128:(n + 1) * 128, ts], in_=st[:, :]), r=[st], final=True)

from concourse.bass_utils import run_bass_kernel_spmd

DEPTH = 4
SEQ = 4096
NB = 4


def _consts(nc, S):
    ones = S.sb([128, 128], F32, "c_ones")
    S.op("pool", lambda e: e.memset(ones[:, :], 1.0), w=[ones])
    identf = S.sb([128, 128], F32, "identf")
    ident = S.sb([128, 128], BF16, "ident")
    S.op("pool", lambda e: e.affine_select(out=identf[:, :], in_=ones[:, :], pattern=[[1, 128]], compare_op=ALU.is_equal, fill=0.0, base=0, channel_multiplier=-1), r=[ones], w=[identf])
    S.op("dve", lambda e: e.tensor_copy(out=ident[:, :], in_=identf[:, :]), r=[identf], w=[ident])
    return dict(ident=ident, identf=identf)


def _level_masks():
    i = np.arange(128)[:, None]
    j = np.arange(128)[None, :]
    m = np.zeros((128, 7, 2, 128), np.float32)
    for lv in range(7):
        b = 1 << lv
        lo = ((i // (2 * b)) == (j // (2 * b))) & ((i % (2 * b)) >= b) & ((j % (2 * b)) < b)
        m[:, lv, 0, :] = lo
        m[:, lv, 1, :] = lo.T
    return m


def build_tok(do_mix, do_proj):
    nc = bass.Bass("TRN2", target_bir_lowering=False)
    d_in = dict(xT=nc.dram_tensor("xT", [D, TOK], F32, kind="ExternalInput").ap(),
                gv=nc.dram_tensor("gv", [128, 2, 8], F32, kind="ExternalInput").ap())
    d_out = {}
    if do_mix:
        d_in["catT"] = nc.dram_tensor("catT", [D, TOK], F32, kind="ExternalInput").ap()
        d_in["w_out"] = nc.dram_tensor("w_out", [D, D], F32, kind="ExternalInput").ap()
        d_in["w_up"] = nc.dram_tensor("w_up", [D, DFF], F32, kind="ExternalInput").ap()
        d_in["w_down"] = nc.dram_tensor("w_down", [DFF, D], F32, kind="ExternalInput").ap()
        d_out["xT"] = nc.dram_tensor("xT_out", [D, TOK], F32, kind="ExternalOutput").ap()
    if do_proj:
        d_in["w_in"] = nc.dram_tensor("w_in", [D, NPROJ], F32, kind="ExternalInput").ap()
        d_out["projT"] = nc.dram_tensor("projT", [NPROJ, TOK], F32, kind="ExternalOutput").ap()
    S = Sched(nc)
    emit_tok(nc, S, d_in, d_out, do_mix, do_proj)
    S.finish()
    return nc


def build_attn():
    nc = bass.Bass("TRN2", target_bir_lowering=False)
    d_in = dict(
        qT=nc.dram_tensor("qT", [512, TQ], F32, kind="ExternalInput").ap(),
        kT=nc.dram_tensor("kT", [512, TK], F32, kind="ExternalInput").ap(),
        vT=nc.dram_tensor("vT", [512, TK], F32, kind="ExternalInput").ap(),
        vld=nc.dram_tensor("vld", [128, NKT], F32, kind="ExternalInput").ap(),
        qkg=nc.dram_tensor("qkg", [128, 2], F32, kind="ExternalInput").ap(),
    )
    out = nc.dram_tensor("attnT", [512, TQ], F32, kind="ExternalOutput").ap()
    S = Sched(nc)
    consts = _consts(nc, S)
    emit_attn(nc, S, d_in, out, consts)
    S.finish()
    return nc


def build_rwkv(T):
    nc = bass.Bass("TRN2", target_bir_lowering=False)
    d_in = dict(
        u_rkv=nc.dram_tensor("u_rkv", [3, 256, T + 2], F32, kind="ExternalInput").ap(),
        u_x=nc.dram_tensor("u_x", [256, T + 2], F32, kind="ExternalInput").ap(),
        cvec=nc.dram_tensor("cvec", [128, 2, NCV], F32, kind="ExternalInput").ap(),
        xvec=nc.dram_tensor("xvec", [128, 2, NXV], F32, kind="ExternalInput").ap(),
        wa2=nc.dram_tensor("wa2", [128, 4, 256], F32, kind="ExternalInput").ap(),
        g2=nc.dram_tensor("g2", [128, 256], F32, kind="ExternalInput").ap(),
        gnwb=nc.dram_tensor("gnwb", [128, 2, 256], F32, kind="ExternalInput").ap(),
        lvm=nc.dram_tensor("lvm", [128, 7, 2, 128], F32, kind="ExternalInput").ap(),
        yacc=nc.dram_tensor("yacc", [T // 128, 128, 256], F32, kind="Internal").ap(),
    )
    out = nc.dram_tensor("rwkvT", [256, T], F32, kind="ExternalOutput").ap()
    S = Sched(nc)
    consts = _consts(nc, S)
    emit_rwkv(nc, S, T, d_in, out, consts)
    S.finish()
    return nc


def _attn_inputs(PT, gq, gk, hf):
    f = np.float32
    g0 = hf * TQ - HALO
    lo = max(g0, 0)
    hi = min(g0 + TK, SEQ)

    def padT(rows):
        out = np.zeros((512, TK), f)
        out[:, lo - g0:hi - g0] = rows[:, lo:hi]
        return out
    pos = g0 + np.arange(TK)
    valid = ((pos >= 0) & (pos < SEQ)).astype(f)
    vld = np.zeros((128, NKT), f)
    for kidx, (d, r, i, m0) in enumerate(KT):
        vld[:, kidx] = valid[r + d * (m0 + np.arange(128))]
    qkg = np.stack([np.tile(gq, 2), np.tile(gk, 2)], 1).astype(f)
    return dict(qT=np.ascontiguousarray(PT[0:512, hf * TQ:(hf + 1) * TQ]), kT=padT(PT[512:1024]), vT=padT(PT[1024:1536]), vld=vld, qkg=qkg)


def _rwkv_inputs(PT, hh, P):
    f = np.float32
    U = PT[1536:]
    cs = slice(hh * 256, (hh + 1) * 256)
    pad = lambda rows: np.pad(rows, ((0, 0), (1, 1)))
    u_rkv = np.stack([pad(U[0:512][cs]), pad(U[512:1024][cs]), pad(U[1024:1536][cs])], 0).astype(f)
    u_x = pad(U[1536:1792]).astype(f)
    sp, sn = P["sp"], P["sn"]
    cvec = np.zeros((128, 2, NCV), f)
    for j in range(2):
        ch = slice(hh * 256 + j * 128, hh * 256 + (j + 1) * 128)
        for q, key in enumerate(("spr", "spk", "spv")):
            a = sp[q * 512:(q + 1) * 512][ch]
            b = sn[q * 512:(q + 1) * 512][ch]
            cvec[:, j, CV[key]] = a
            cvec[:, j, CV[key] + 1] = b
            cvec[:, j, CV[key] + 2] = np.float32(1.0) - a - b
        cvec[:, j, CV["kk"]] = P["k_k"][ch]
        cvec[:, j, CV["ka"]] = P["k_a"][ch]
        cvec[:, j, CV["rk"]] = P["r_k"].reshape(-1)[ch]
        for d in range(2):
            cvec[:, j, CV["w00"] + d] = P["w0"][d][ch]
            cvec[:, j, CV["a00"] + d] = P["a0"][d][ch]
    xvec = np.zeros((128, 2, NXV), f)
    for i in range(2):
        a = sp[1536 + i * 128:1536 + (i + 1) * 128]
        b = sn[1536 + i * 128:1536 + (i + 1) * 128]
        xvec[:, i, 0] = a
        xvec[:, i, 1] = b
        xvec[:, i, 2] = np.float32(1.0) - a - b
    wa2 = np.zeros((128, 4, 256), f)
    for d in range(2):
        wa2[0:64, d] = P["w2"][d][:, cs]
        wa2[64:128, 2 + d] = P["a2"][d][:, cs]
    gnwb = np.stack([np.broadcast_to(P["gn_w"][cs], (128, 256)), np.broadcast_to(P["gn_b"][cs], (128, 256))], 1).astype(f)
    return dict(u_rkv=np.ascontiguousarray(u_rkv), u_x=np.ascontiguousarray(u_x), cvec=cvec, xvec=xvec, wa2=wa2,
                g2=np.ascontiguousarray(P["g2"][:, cs]).astype(f), gnwb=np.ascontiguousarray(gnwb), lvm=_level_masks())


_PROGS = {}


def _prog(name, fn):
    if name not in _PROGS:
        _PROGS[name] = fn()
    return _PROGS[name]


def kernel(x, ln1_g, w_in, q_norm_g, k_norm_g, tshift_prev, tshift_next, rwkv_w0, rwkv_w2,
           rwkv_a0, rwkv_a2, rwkv_g2, rwkv_k_k, rwkv_k_a, rwkv_r_k, rwkv_gn_w, rwkv_gn_b,
           w_out, ln2_g, w_up, w_down):
    f = np.float32
    A = lambda a: np.ascontiguousarray(np.asarray(a, f))
    x = A(x)
    ln1_g, ln2_g = A(ln1_g), A(ln2_g)
    cores = list(range(8))
    gcol = lambda g: g.reshape(8, 128).T
    xT = [np.ascontiguousarray(x[c // 2, (c % 2) * TOK:(c % 2 + 1) * TOK, :].T) for c in cores]

    gv = np.ascontiguousarray(np.stack([gcol(ln2_g[0]), gcol(ln1_g[0])], 1))
    res = run_bass_kernel_spmd(_prog("tokP", lambda: build_tok(False, True)),
                               [dict(xT=xT[c], gv=gv, w_in=A(w_in[0])) for c in cores], core_ids=cores)
    projT = [np.asarray(res.results[c]["projT"], f) for c in cores]

    for l in range(DEPTH):
        PT = [np.concatenate([projT[2 * b], projT[2 * b + 1]], axis=1) for b in range(NB)]
        gq, gk = A(q_norm_g[l]), A(k_norm_g[l])
        res = run_bass_kernel_spmd(_prog("attn", build_attn), [_attn_inputs(PT[c // 2], gq, gk, c % 2) for c in cores], core_ids=cores)
        attnT = [np.asarray(res.results[c]["attnT"], f) for c in cores]
        P = dict(sp=A(tshift_prev[l]), sn=A(tshift_next[l]), w0=A(rwkv_w0[l]), w2=A(rwkv_w2[l]), a0=A(rwkv_a0[l]), a2=A(rwkv_a2[l]),
                 g2=A(rwkv_g2[l]), k_k=A(rwkv_k_k[l]), k_a=A(rwkv_k_a[l]), r_k=A(rwkv_r_k[l]), gn_w=A(rwkv_gn_w[l]), gn_b=A(rwkv_gn_b[l]))
        res = run_bass_kernel_spmd(_prog("rwkv", lambda: build_rwkv(SEQ)), [_rwkv_inputs(PT[c // 2], c % 2, P) for c in cores], core_ids=cores)
        rwkvT = [np.asarray(res.results[c]["rwkvT"], f) for c in cores]
        last = (l == DEPTH - 1)
        ins = []
        for c in cores:
            b, hf = c // 2, c % 2
            ts = slice(hf * TOK, (hf + 1) * TOK)
            catT = np.concatenate([attnT[c], rwkvT[2 * b][:, ts], rwkvT[2 * b + 1][:, ts]], axis=0)
            gv = np.ascontiguousarray(np.stack([gcol(ln2_g[l]), gcol(ln1_g[min(l + 1, DEPTH - 1)])], 1))
            d = dict(xT=xT[c], gv=gv, catT=np.ascontiguousarray(catT), w_out=A(w_out[l]), w_up=A(w_up[l]), w_down=A(w_down[l]))
            if not last:
                d["w_in"] = A(w_in[l + 1])
            ins.append(d)
        if last:
            res = run_bass_kernel_spmd(_prog("tokM", lambda: build_tok(True, False)), ins, core_ids=cores)
        else:
            res = run_bass_kernel_spmd(_prog("tokMP", lambda: build_tok(True, True)), ins, core_ids=cores)
            projT = [np.asarray(res.results[c]["projT"], f) for c in cores]
        xT = [np.asarray(res.results[c]["xT_out"], f) for c in cores]

    out = np.empty((NB, SEQ, D), f)
    for c in cores:
        out[c // 2, (c % 2) * TOK:(c % 2 + 1) * TOK, :] = xT[c].T
    return out
```

```python
import contextlib
import concourse.bass as bass
import concourse.mybir as mybir

F32 = mybir.dt.float32
BF16 = mybir.dt.bfloat16
I32 = mybir.dt.int32
AF = mybir.ActivationFunctionType
ALU = mybir.AluOpType
AX = mybir.AxisListType

ENGS = ("pe", "act", "dve", "pool", "sp")


class Reg:
    __slots__ = ("name", "w", "r")

    def __init__(self, name=""):
        self.name = name
        self.w = None
        self.r = []


class Tile:
    def __init__(self, t, name):
        self.t = t
        self.reg = Reg(name)

    def __getitem__(self, idx):
        return self.t[idx]


class _Rec:
    def __init__(self):
        self.calls = []

    def __getattr__(self, name):
        def f(*a, **k):
            self.calls.append((name, a, k))
            return self
        return f


def _capture(fn):
    rec = _Rec()
    fn(rec)
    assert len(rec.calls) == 1, rec.calls
    name, a, k = rec.calls[0]
    return lambda e: getattr(e, name)(*a, **k)


class Sched:
    def __init__(self, nc, n_dma_sems=24, strict_same=True):
        self.nc = nc
        self.stack = contextlib.ExitStack()
        self.strict_same = strict_same
        import os as _os
        self.tailwait = _os.environ.get("TAILWAIT", "1") != "0"
        self.prog = {e: [] for e in ENGS}
        self.sems = []
        self.eng_sem = {}
        self.cnt = {}
        for e in ENGS:
            self.eng_sem[e] = self._new_sem("s_" + e)
            self.cnt[e] = 0
        self.dma_ring = {}
        self.dma_pos = {}
        self.dma_val = {}
        for e in ("sp", "pool", "act"):
            ring = [self._new_sem("d_%s_%d" % (e, i)) for i in range(n_dma_sems)]
            self.dma_ring[e] = ring
            self.dma_pos[e] = 0
        self.semval = [0] * len(self.sems)
        self.observed = {e: {} for e in ENGS}
        self.ninst = 0
        self.final_toks = []
        self.waited = {}

    def _new_sem(self, name):
        s = self.stack.enter_context(self.nc.semaphore(name))
        self.sems.append(s)
        return len(self.sems) - 1

    def sb(self, shape, dtype, name):
        t = self.stack.enter_context(self.nc.sbuf_tensor(name, list(shape), dtype))
        return Tile(t, name)

    def ps(self, shape, dtype, name):
        t = self.stack.enter_context(self.nc.psum_tensor(name, list(shape), dtype))
        return Tile(t, name)

    def _need(self, eng, tok, waits):
        if tok is None:
            return
        sem, val, teng = tok
        if teng == eng and (eng == "pe" or not self.strict_same) and sem == self.eng_sem[eng]:
            return
        if self.observed[eng].get(sem, 0) >= val:
            return
        if self.tailwait and teng in self.cnt and teng != eng:
            val = self.cnt[teng]
        if waits.get(sem, 0) < val:
            waits[sem] = val

    def _deps(self, eng, r, w):
        waits = {}
        for reg in r:
            self._need(eng, reg.w, waits)
        for reg in w:
            self._need(eng, reg.w, waits)
            for tok in reg.r:
                self._need(eng, tok, waits)
        return waits

    def _emit_waits(self, eng, waits):
        for sem, val in waits.items():
            self.observed[eng][sem] = val
            self.prog[eng].append(("wait", sem, val))
            self.waited.setdefault(sem, set()).add(val)

    def _mark(self, tok, r, w):
        for reg in r:
            reg.r.append(tok)
        for reg in w:
            reg.w = tok
            reg.r = []

    def _regs(self, lst):
        out = []
        for x in lst:
            if isinstance(x, Tile):
                out.append(x.reg)
            elif isinstance(x, Reg):
                out.append(x)
            elif x is None:
                continue
            else:
                out.extend(self._regs(x))
        return out

    def op(self, eng, fn, r=(), w=()):
        fn = _capture(fn)
        r = self._regs(r)
        w = self._regs(w)
        waits = self._deps(eng, r, w)
        self._emit_waits(eng, waits)
        sem = self.eng_sem[eng]
        self.cnt[eng] += 1
        val = self.cnt[eng]
        s = self.sems[sem]
        self.prog[eng].append(("op", fn, sem, val))
        tok = (sem, val, eng)
        self._mark(tok, r, w)
        self.ninst += 1
        return tok

    def dma(self, eng, fn, r=(), w=(), final=False):
        fn = _capture(fn)
        r = self._regs(r)
        w = self._regs(w)
        waits = self._deps(eng, r, w)
        ring = self.dma_ring[eng]
        pos = self.dma_pos[eng]
        self.dma_pos[eng] = (pos + 1) % len(ring)
        sem = ring[pos]
        prev = self.semval[sem]
        if prev > 0 and self.observed[eng].get(sem, 0) < prev:
            if waits.get(sem, 0) < prev:
                waits[sem] = prev
        self._emit_waits(eng, waits)
        val = prev + 16
        self.semval[sem] = val
        s = self.sems[sem]
        self.prog[eng].append(("dma", fn, sem, val))
        tok = (sem, val, "dma_" + eng)
        self._mark(tok, r, w)
        self.ninst += 1
        if final:
            self.final_toks.append(tok)
        return tok

    def barrier(self, engs=("pe", "act", "dve", "pool")):
        for e in engs:
            waits = {}
            for o in engs:
                if o == e:
                    continue
                sem = self.eng_sem[o]
                val = self.cnt[o]
                if val > 0 and self.observed[e].get(sem, 0) < val:
                    waits[sem] = val
            self._emit_waits(e, waits)

    def wait_all(self, eng, regs):
        regs = self._regs(regs)
        waits = {}
        for reg in regs:
            if reg.w is not None:
                sem, val, teng = reg.w
                if self.observed[eng].get(sem, 0) < val and waits.get(sem, 0) < val:
                    waits[sem] = val
        self._emit_waits(eng, waits)

    def finish(self):
        nc = self.nc
        prog = self.prog
        waits = {}
        for sem, val, _ in self.final_toks:
            if self.observed["sp"].get(sem, 0) < val and waits.get(sem, 0) < val:
                waits[sem] = val
        self._emit_waits("sp", waits)
        eng_sems = set(self.eng_sem.values())
        idx = {}
        for sem in eng_sems:
            vals = sorted(self.waited.get(sem, ()))
            idx[sem] = {v: i + 1 for i, v in enumerate(vals)}
        sems = self.sems
        self.n_inc = sum(len(v) for v in idx.values())

        def replay(e, items):
            for it in items:
                if it[0] == "wait":
                    _, sem, val = it
                    if sem in eng_sems:
                        val = idx[sem][val]
                    e.wait_ge(sems[sem], val)
                elif it[0] == "op":
                    _, fn, sem, val = it
                    ins = fn(e)
                    if val in idx[sem]:
                        ins.then_inc(sems[sem], 1)
                else:
                    _, fn, sem, val = it
                    fn(e).then_inc(sems[sem], 16)

        with nc.Block() as block:
            @block.tensor
            def _(e):
                replay(e, prog["pe"])

            @block.scalar
            def _(e):
                replay(e, prog["act"])

            @block.vector
            def _(e):
                replay(e, prog["dve"])

            @block.gpsimd
            def _(e):
                replay(e, prog["pool"])

            @block.sync
            def _(e):
                replay(e, prog["sp"])
        self.stack.close()

import numpy as np
import os

C = 128
SBK = 512
NCH = SBK // C
E05 = float(np.exp(-0.5))
GN_EPS = 64 * 1e-5
import os
STAGE = int(os.environ.get('STAGE', '99'))
SUB = os.environ.get('SUB', 'ab')
SUB5 = os.environ.get('SUB5', 'qgh')
SUB6 = os.environ.get('SUB6', 'abc')

CV = dict(spr=0, snr=1, c0r=2, spk=3, snk=4, c0k=5, spv=6, snv=7, c0v=8, kk=9, ka=10, rk=11,
          w00=12, w01=13, a00=14, a01=15)
NCV = 16
NXV = 3


DBG = {}
def dbg(nc, S, name, tile, ap_fn, shape, dtype=None):
    if os.environ.get("DBG") is None or name in DBG:
        return
    if os.environ.get("DBG") == "f32" and dtype is not None:
        return
    if os.environ.get("DBG") not in ("1", "f32") and name not in os.environ.get("DBG").split(","):
        return
    dt = dtype or F32
    d = nc.dram_tensor("dbg_" + name, list(shape), dt, kind="ExternalOutput").ap()
    DBG[name] = d
    S.dma("sp", lambda e: e.dma_start(out=d, in_=ap_fn()), r=[tile], final=True)


def emit_rwkv(nc, S, T, d_in, d_out, consts, NJ=2, dirs=(0, 1)):
    u_rkv = d_in["u_rkv"]
    u_x = d_in["u_x"]
    cvec = d_in["cvec"]
    xvec = d_in["xvec"]
    wa2 = d_in["wa2"]
    g2 = d_in["g2"]
    gnwb = d_in["gnwb"]
    yacc_d = d_in["yacc"]
    nsb = T // SBK
    ident = consts["ident"]
    identf = consts["identf"]

    cv = S.sb([128, NJ, NCV], F32, "cv")
    xv = S.sb([128, 2, NXV], F32, "xv")
    wa2_sb = S.sb([128, 4, 128 * NJ], BF16, "wa2_sb")
    g2_sb = S.sb([128, 128 * NJ], BF16, "g2_sb")
    gnwb_sb = S.sb([128, 2, 128 * NJ], F32, "gnwb_sb")
    S.dma("sp", lambda e: e.dma_start(out=cv[:, :, :], in_=cvec[:, :, :]), w=[cv])
    S.dma("sp", lambda e: e.dma_start(out=xv[:, :, :], in_=xvec[:, :, :]), w=[xv])
    S.dma("pool", lambda e: e.dma_start(out=wa2_sb[:, :, :], in_=wa2[:, :, :]), w=[wa2_sb])
    S.dma("pool", lambda e: e.dma_start(out=g2_sb[:, :], in_=g2[:, :]), w=[g2_sb])
    S.dma("sp", lambda e: e.dma_start(out=gnwb_sb[:, :, :], in_=gnwb[:, :, :]), w=[gnwb_sb])

    for _i in range(int(os.environ.get("DUMMY", "0"))):
        S.dma("sp", lambda e: e.dma_start(out=xv[:, :, :], in_=xvec[:, :, :]), w=[xv])
    ones_f = S.sb([128, 256], F32, "ones_f")
    S.op("pool", lambda e: e.memset(ones_f[:, :], 1.0), w=[ones_f])
    bones = S.sb([128, 128], BF16, "bones")
    S.op("pool", lambda e: e.memset(bones[:, :], 0.0), w=[bones])
    S.op("pool", lambda e: e.memset(bones[0:64, 0:64], 1.0), w=[bones])
    S.op("pool", lambda e: e.memset(bones[64:128, 64:128], 1.0), w=[bones])
    mk = {}
    for name, cm, st, cmp in (("SF", -1, 1, ALU.is_gt), ("IF", -1, 1, ALU.is_ge), ("SR", 1, -1, ALU.is_gt), ("IR", 1, -1, ALU.is_ge)):
        m = S.sb([128, 128], F32, "mk_" + name)
        S.op("pool", lambda e, m=m, cm=cm, st=st, cmp=cmp: e.affine_select(out=m[:, :], in_=ones_f[:, 0:128], pattern=[[st, 128]], compare_op=cmp, fill=0.0, base=0, channel_multiplier=cm), r=[ones_f], w=[m])
        mk[name] = m
    mk2 = []
    for d, (a_, b_) in enumerate((("SF", "IF"), ("SR", "IR"))):
        m = S.sb([128, 256], F32, "mk2_%d" % d)
        S.op("dve", lambda e, m=m, a_=a_: e.tensor_copy(out=m[:, 0:128], in_=mk[a_][:, :]), r=[mk[a_]], w=[m])
        S.op("dve", lambda e, m=m, b_=b_: e.tensor_copy(out=m[:, 128:256], in_=mk[b_][:, :]), r=[mk[b_]], w=[m])
        mk2.append(m)
    mkL = [mk["SR"], mk["SF"]]
    eps12 = S.sb([128, 1], F32, "eps12")
    S.op("pool", lambda e: e.memset(eps12[:, :], 1e-12), w=[eps12])
    epsgn = S.sb([128, 1], F32, "epsgn")
    S.op("pool", lambda e: e.memset(epsgn[:, :], GN_EPS), w=[epsgn])
    rmask = S.sb([128, SBK], F32, "rmask")
    S.op("pool", lambda e: e.memset(rmask[:, :], 1.0), w=[rmask])
    for c in range(NCH):
        S.op("pool", lambda e, c=c: e.memset(rmask[:, c * C:c * C + 1], 0.0), w=[rmask])

    W = SBK + 2
    ux = [S.sb([128, W], F32, "ux%d" % i) for i in range(2)]
    urkv = [[S.sb([128, W], F32, "urkv%d_%d" % (q, j)) for j in range(2)] for q in range(3)]
    tmp = S.sb([128, SBK], F32, "tmp")
    dbgt = S.sb([128, SBK], F32, "dbgt")
    dbgt2 = S.sb([128, SBK], F32, "dbgt2")
    dbgt3 = S.sb([128, SBK], F32, "dbgt3")
    xs = [S.sb([128, SBK], F32, "xs%d" % i) for i in range(2)]
    xb = [S.sb([128, SBK], BF16, "xb%d" % i) for i in range(2)]
    rkv = [[S.sb([128, SBK], F32, "rkv%d_%d" % (q, j)) for j in range(2)] for q in range(3)]
    kkt = [S.sb([128, SBK], F32, "kk%d" % j) for j in range(2)]
    sqb = S.sb([128, SBK], BF16, "sqb")
    a_t = [[S.sb([128, SBK], F32, "a%d_%d" % (d, j)) for j in range(2)] for d in range(2)]
    lw = [S.sb([128, SBK], F32, "lw%d" % j) for j in range(2)]
    cp = [S.sb([128, SBK], F32, "cp%d" % j) for j in range(2)]
    ce = [S.sb([128, SBK], F32, "ce%d" % j) for j in range(2)]
    kt = [S.sb([128, SBK], F32, "kt%d" % j) for j in range(2)]
    kts = [S.sb([128, SBK], F32, "kts%d" % j) for j in range(2)]
    akk = [S.sb([128, SBK], F32, "akk%d" % j) for j in range(2)]
    ex = [S.sb([128, SBK], F32, "ex%d" % i) for i in range(4)]
    fm = {n: [S.sb([128, SBK], BF16, "fm_%s%d" % (n, j)) for j in range(2)] for n in ("At", "Bt", "Bpt", "Kt", "Kpt", "Rt", "Vt")}
    fmo = {n: [S.sb([64, SBK], BF16, "fmo_%s%d" % (n, j)) for j in range(2)] for n in ("At", "Bt", "Bpt", "Kt", "Kpt", "Rt", "Vt")}
    PCo = [S.sb([64, NCH], F32, "PCo%d" % j) for j in range(2)]
    g_t = [S.sb([128, SBK], F32, "g%d" % j) for j in range(2)]
    PCc = [S.sb([128, NCH], F32, "PCc%d" % j) for j in range(2)]
    yacc_reg = [Reg("yacc%d" % i) for i in range(T // C)]
    bon = [S.sb([128, SBK], F32, "bon%d" % j) for j in range(2)]

    ps_prep = [S.ps([128, 512], F32, "ps_prep%d" % i) for i in range(2)]
    bankP0 = [S.ps([128, 512], F32, "bankP0_%d" % e) for e in range(2)]
    _bp1 = S.ps([128, 512], F32, "bankP1")
    bankD = S.ps([128, 512], F32, "bankD")
    bankS0 = S.ps([128, 512], F32, "bankS0")
    bankS1 = S.ps([128, 512], F32, "bankS1")
    def sub(bank, lo, hi, name):
        return (bank, lo, hi, Reg(name))
    PS1 = [sub(bankP0[e], 0, 256, "PS1") for e in range(2)]
    PS2 = [sub(bankP0[e], 256, 512, "PS2") for e in range(2)]
    PS3 = [sub(_bp1, 256 * e, 256 * e + 128, "PS3") for e in range(2)]
    PS4 = [sub(_bp1, 256 * e + 128, 256 * e + 256, "PS4") for e in range(2)]
    PSY = [sub(bankD, 64 * e, 64 * e + 64, "PSY") for e in range(2)]
    PS5 = sub(bankS0, 0, 64, "PS5")
    PSz = sub(bankS0, 64, 192, "PSz")
    PSn = sub(bankS1, 0, 128, "PSn")
    PSl = sub(bankS1, 128, 256, "PSl")
    PSq = sub(bankS0, 192, 320, "PSq")
    PSg = sub(bankS0, 320, 384, "PSg")
    PSh = sub(bankS1, 256, 320, "PSh")
    _pss = sub(bankD, 128, 192, "PSs")
    PSs = [_pss, _pss]
    PStr = sub(bankD, 192, 320, "PStr")

    def pa(s_, p0=0, p1=128, lo=None, hi=None):
        bank, a, b, _ = s_
        lo = a if lo is None else a + lo
        hi = b if hi is None else a + hi
        return bank.t[p0:p1, lo:hi]

    X1 = [S.sb([128, 256], BF16, "X1_%d" % e) for e in range(2)]
    X2 = [S.sb([128, 256], BF16, "X2_%d" % e) for e in range(2)]
    Lm = [[S.sb([128, 128], BF16, "L_%d_%d" % (e, i)) for i in range(2)] for e in range(2)]
    Nm = [[S.sb([128, 128], BF16, "N_%d_%d" % (e, i)) for i in range(2)] for e in range(2)]
    TM = [S.sb([128, 256], BF16, "TM_%d" % e) for e in range(2)]
    Z = [[S.sb([128, 128], BF16, "Z_%d_%d" % (e, i)) for i in range(2)] for e in range(2)]
    Lmk = [[S.sb([128, 128], BF16, "Lmk_%d_%d" % (e, lv)) for lv in range(7)] for e in range(2)]
    XT = [S.sb([128, 128], BF16, "XT_%d" % e) for e in range(2)]
    Tbuf = [[S.sb([128, 128], BF16, "Tb_%d_%d" % (e, i)) for i in range(2)] for e in range(2)]
    TTbuf = [[S.sb([128, 128], BF16, "TbT_%d_%d" % (e, i)) for i in range(2)] for e in range(2)]
    lvm = S.sb([128, 7, 2, 128], F32, "lvm_sb")
    S.dma("sp", lambda e: e.dma_start(out=lvm[:, :, :, :], in_=d_in["lvm"][:, :, :, :]), w=[lvm])
    Qt = [S.sb([128, 128], BF16, "Qt_%d" % e) for e in range(2)]
    Gt = [S.sb([128, 128], BF16, "Gt_%d" % e) for e in range(2)]
    for e in range(2):
        S.op("pool", lambda e_, e=e: e_.memset(Gt[e][:, :], 0.0), w=[Gt[e]])
        S.op("pool", lambda e_, e=e: e_.memset(Qt[e][:, :], 0.0), w=[Qt[e]])
    Hs = [S.sb([128, 64], F32, "Hs_%d" % e) for e in range(2)]
    Sth = [[[S.sb([128, 64], BF16, "St_%d_%d_%d" % (j, i, e)) for e in range(2)] for i in range(2)] for j in range(2)]
    ytok = S.sb([128, 128 * NJ], F32, "ytok")
    ytok_reg = [Reg("ytok%d" % h) for h in range(2 * NJ)]
    yprev = S.sb([128, 128 * NJ], F32, "yprev")
    stats = S.sb([128, 2 * NJ, 6], F32, "stats")
    mv = S.sb([128, 2 * NJ, 2], F32, "mv")
    rstd = S.sb([128, 2 * NJ], F32, "rstd")
    yn = S.sb([128, 128 * NJ], F32, "yn")
    yo = [S.sb([128, SBK], F32, "yo%d" % j) for j in range(2)]

    def shift(dst, src, vt, vi, cols, eng="dve"):
        sp_, sn_, c0_ = cols
        vec = vt.t[:, vi, :]
        S.op("dve", lambda e: e.tensor_scalar(out=tmp[:, :], in0=src[:, 0:SBK], scalar1=vec[:, sp_:sp_ + 1], scalar2=None, op0=ALU.mult), r=[src, vt], w=[tmp])
        S.op("dve", lambda e: e.scalar_tensor_tensor(out=tmp[:, :], in0=src[:, 2:SBK + 2], scalar=vec[:, sn_:sn_ + 1], in1=tmp[:, :], op0=ALU.mult, op1=ALU.add), r=[src, vt, tmp], w=[tmp])
        S.op("dve", lambda e: e.scalar_tensor_tensor(out=dst[:, :], in0=src[:, 1:SBK + 1], scalar=vec[:, c0_:c0_ + 1], in1=tmp[:, :], op0=ALU.mult, op1=ALU.add), r=[src, vt, tmp], w=[dst])

    NH = 2 * NJ
    CW = 128 * NJ
    for d in list(dirs) * int(os.environ.get('NREP', '1')):
        sb_order = list(range(nsb)) if d == 0 else list(range(nsb - 1, -1, -1))
        for j in range(NJ):
            for e in range(2):
                S.op("pool", lambda e_, j=j, e=e: e_.memset(Sth[j][0][e][:, :], 0.0), w=[Sth[j][0][e]])
                S.op("pool", lambda e_, j=j, e=e: e_.memset(Sth[j][1][e][:, :], 0.0), w=[Sth[j][1][e]])
        cur = [[0, 0], [0, 0]]
        for sb in sb_order:
            t0 = sb * SBK
            for i in range(2):
                S.dma("sp", lambda e, i=i: e.dma_start(out=ux[i][:, :], in_=u_x[i * 128:(i + 1) * 128, t0:t0 + W]), w=[ux[i]])
            for q in range(3):
                for j in range(NJ):
                    S.dma("sp", lambda e, q=q, j=j: e.dma_start(out=urkv[q][j][:, :], in_=u_rkv[q, j * 128:(j + 1) * 128, t0:t0 + W]), w=[urkv[q][j]])
            for i in range(2):
                shift(xs[i], ux[i], xv, i, (0, 1, 2))
            for q in range(3):
                for j in range(NJ):
                    base = q * 3
                    shift(rkv[q][j], urkv[q][j], cv, j, (base, base + 1, base + 2))
            S.op("act", lambda e: e.activation(out=xb[0][0:64, :], in_=xs[0][0:64, :], func=AF.Tanh), r=[xs[0]], w=[xb[0]])
            S.op("act", lambda e: e.copy(out=xb[0][64:128, :], in_=xs[0][64:128, :]), r=[xs[0]], w=[xb[0]])
            for j in range(NJ):
                S.op("pe", lambda e, j=j: e.matmul(ps_prep[0][:, :], lhsT=wa2_sb[:, d, j * 128:(j + 1) * 128], rhs=xb[0][:, :], start=True, stop=True), r=[wa2_sb, xb[0]], w=[ps_prep[0]])
                S.op("act", lambda e, j=j: e.activation(out=lw[j][:, :], in_=ps_prep[0][:, :], func=AF.Sigmoid, bias=cv[:, j, CV["w00"] + d:CV["w00"] + d + 1]), r=[ps_prep[0], cv], w=[lw[j]])
                S.op("pe", lambda e, j=j: e.matmul(ps_prep[1][:, :], lhsT=wa2_sb[:, 2 + d, j * 128:(j + 1) * 128], rhs=xb[0][:, :], start=True, stop=True), r=[wa2_sb, xb[0]], w=[ps_prep[1]])
                if d == 0 and sb == 0 and j == 0 and os.environ.get("DBG"):
                    S.op("dve", lambda e: e.tensor_copy(out=dbgt[:, :], in_=ps_prep[1][:, :]), r=[ps_prep[1]], w=[dbgt])
                    dbg(nc, S, "psa", dbgt, lambda: dbgt.t[:, :], [128, SBK])
                    dbg(nc, S, "xs0", xs[0], lambda: xs[0].t[:, :], [128, SBK])
                    S.op("dve", lambda e: e.tensor_copy(out=dbgt2[:, :], in_=wa2_sb.t[:, :, :].rearrange("p a b -> p (a b)")), r=[wa2_sb], w=[dbgt2])
                    dbg(nc, S, "wa2", dbgt2, lambda: dbgt2.t[:, :], [128, SBK])
                    S.op("dve", lambda e: e.tensor_copy(out=dbgt3[:, :], in_=xb[0][:, :]), r=[xb[0]], w=[dbgt3])
                    dbg(nc, S, "xb0", dbgt3, lambda: dbgt3.t[:, :], [128, SBK])
                    dbg(nc, S, "cv", cv, lambda: cv.t[:, :, :], [128, 2, NCV])
                S.op("act", lambda e, j=j: e.activation(out=a_t[d][j][:, :], in_=ps_prep[1][:, :], func=AF.Sigmoid, bias=cv[:, j, CV["a00"] + d:CV["a00"] + d + 1]), r=[ps_prep[1], cv], w=[a_t[d][j]])
                if d == 1:
                    S.op("pe", lambda e, j=j: e.matmul(ps_prep[1][:, :], lhsT=wa2_sb[:, 2, j * 128:(j + 1) * 128], rhs=xb[0][:, :], start=True, stop=True), r=[wa2_sb, xb[0]], w=[ps_prep[1]])
                    S.op("act", lambda e, j=j: e.activation(out=a_t[0][j][:, :], in_=ps_prep[1][:, :], func=AF.Sigmoid, bias=cv[:, j, CV["a00"]:CV["a00"] + 1]), r=[ps_prep[1], cv], w=[a_t[0][j]])
                S.op("dve", lambda e, j=j: e.tensor_scalar(out=lw[j][:, :], in0=lw[j][:, :], scalar1=-E05, scalar2=None, op0=ALU.mult), r=[lw[j]], w=[lw[j]])
                S.op("dve", lambda e, j=j: e.tensor_tensor_scan(out=cp[j][:, :], data0=rmask[:, :], data1=lw[j][:, :], initial=0.0, op0=ALU.mult, op1=ALU.add), r=[rmask, lw[j]], w=[cp[j]])
                if d == 1:
                    def f_suf(e, j=j):
                        v = cp[j].t[:, :].rearrange("p (c f) -> p c f", f=C)
                        last = v[:, :, C - 1:C].broadcast_to([128, NCH, C])
                        return e.tensor_tensor(out=tmp.t[:, :].rearrange("p (c f) -> p c f", f=C), in0=last, in1=v, op=ALU.subtract)
                    S.op("dve", f_suf, r=[cp[j]], w=[tmp])
                    S.op("dve", lambda e, j=j: e.tensor_tensor(out=cp[j][:, :], in0=tmp[:, :], in1=lw[j][:, :], op=ALU.add), r=[tmp, lw[j]], w=[cp[j]])
                S.op("dve", lambda e, j=j: e.tensor_tensor(out=ce[j][:, :], in0=cp[j][:, :], in1=lw[j][:, :], op=ALU.subtract), r=[cp[j], lw[j]], w=[ce[j]])
                S.op("dve", lambda e, j=j: e.tensor_scalar(out=kkt[j][:, :], in0=rkv[1][j][:, :], scalar1=cv[:, j, CV["kk"]:CV["kk"] + 1], scalar2=None, op0=ALU.mult), r=[rkv[1][j], cv], w=[kkt[j]])
                S.op("act", lambda e, j=j: e.activation(out=sqb[:, :], in_=kkt[j][:, :], func=AF.Square), r=[kkt[j]], w=[sqb])
                S.op("pe", lambda e: e.matmul(ps_prep[0][:, :], lhsT=bones[:, :], rhs=sqb[:, :], start=True, stop=True), r=[bones, sqb], w=[ps_prep[0]])
                S.op("act", lambda e: e.activation(out=tmp[:, :], in_=ps_prep[0][:, :], func=AF.Ln, bias=eps12[:, 0:1]), r=[ps_prep[0], eps12], w=[tmp])
                S.op("act", lambda e: e.activation(out=tmp[:, :], in_=tmp[:, :], func=AF.Exp, scale=-0.5), r=[tmp], w=[tmp])
                S.op("dve", lambda e, j=j: e.tensor_tensor(out=kkt[j][:, :], in0=kkt[j][:, :], in1=tmp[:, :], op=ALU.mult), r=[kkt[j], tmp], w=[kkt[j]])
                def f_kt(dd, dst):
                    S.op("dve", lambda e, j=j: e.tensor_scalar(out=tmp[:, :], in0=a_t[dd][j][:, :], scalar1=-1.0, scalar2=cv[:, j, CV["ka"]:CV["ka"] + 1], op0=ALU.add, op1=ALU.mult), r=[a_t[dd][j], cv], w=[tmp])
                    S.op("dve", lambda e, j=j: e.scalar_tensor_tensor(out=dst[:, :], in0=tmp[:, :], scalar=1.0, in1=rkv[1][j][:, :], op0=ALU.add, op1=ALU.mult), r=[tmp, rkv[1][j]], w=[dst])
                f_kt(d, kt[j])
                S.op("dve", lambda e, j=j: e.tensor_tensor(out=akk[j][:, :], in0=a_t[d][j][:, :], in1=kkt[j][:, :], op=ALU.mult), r=[a_t[d][j], kkt[j]], w=[akk[j]])
                S.op("act", lambda e, j=j: e.activation(out=ex[0][:, :], in_=ce[j][:, :], func=AF.Exp), r=[ce[j]], w=[ex[0]])
                S.op("act", lambda e, j=j: e.activation(out=ex[1][:, :], in_=cp[j][:, :], func=AF.Exp, scale=-1.0), r=[cp[j]], w=[ex[1]])
                S.op("act", lambda e, j=j: e.activation(out=ex[3][:, :], in_=cp[j][:, :], func=AF.Exp), r=[cp[j]], w=[ex[3]])
                lidx = C - 1 if d == 0 else 0
                S.op("act", lambda e, j=j, lidx=lidx: e.activation(out=PCc[j].t[:, :].rearrange("p (c o) -> p c o", o=1), in_=cp[j].t[:, :].rearrange("p (c f) -> p c f", f=C)[:, :, lidx:lidx + 1], func=AF.Exp), r=[cp[j]], w=[PCc[j]])
                for c in range(NCH):
                    li = c * C + (C - 1 if d == 0 else 0)
                    S.op("act", lambda e, j=j, c=c, li=li: e.activation(out=ex[2][:, c * C:(c + 1) * C], in_=cp[j][:, c * C:(c + 1) * C], func=AF.Exp, scale=-1.0, bias=cp[j][:, li:li + 1]), r=[cp[j]], w=[ex[2]])
                S.op("dve", lambda e, j=j: e.scalar_tensor_tensor(out=fm["At"][j][:, :], in0=kkt[j][:, :], scalar=-1.0, in1=ex[0][:, :], op0=ALU.mult, op1=ALU.mult), r=[kkt[j], ex[0]], w=[fm["At"][j]])
                S.op("pool", lambda e, j=j: e.tensor_tensor(out=fm["Bt"][j][:, :], in0=akk[j][:, :], in1=ex[1][:, :], op=ALU.mult), r=[akk[j], ex[1]], w=[fm["Bt"][j]])
                S.op("pool", lambda e, j=j: e.tensor_tensor(out=fm["Bpt"][j][:, :], in0=akk[j][:, :], in1=ex[2][:, :], op=ALU.mult), r=[akk[j], ex[2]], w=[fm["Bpt"][j]])
                S.op("dve", lambda e, j=j: e.tensor_tensor(out=fm["Kt"][j][:, :], in0=kt[j][:, :], in1=ex[1][:, :], op=ALU.mult), r=[kt[j], ex[1]], w=[fm["Kt"][j]])
                S.op("pool", lambda e, j=j: e.tensor_tensor(out=fm["Kpt"][j][:, :], in0=kt[j][:, :], in1=ex[2][:, :], op=ALU.mult), r=[kt[j], ex[2]], w=[fm["Kpt"][j]])
                S.op("pool", lambda e, j=j: e.tensor_tensor(out=fm["Rt"][j][:, :], in0=rkv[0][j][:, :], in1=ex[3][:, :], op=ALU.mult), r=[rkv[0][j], ex[3]], w=[fm["Rt"][j]])
                S.op("act", lambda e, j=j: e.copy(out=fm["Vt"][j][:, :], in_=rkv[2][j][:, :]), r=[rkv[2][j]], w=[fm["Vt"][j]])
                for n in ("At", "Bt", "Bpt", "Kt", "Kpt", "Rt", "Vt"):
                    S.op("dve", lambda e, j=j, n=n: e.tensor_copy(out=fmo[n][j][:, :], in_=fm[n][j][64:128, :]), r=[fm[n][j]], w=[fmo[n][j]])
                S.op("dve", lambda e, j=j: e.tensor_copy(out=PCo[j][:, :], in_=PCc[j][64:128, :]), r=[PCc[j]], w=[PCo[j]])
                if d == 1:
                    f_kt(0, kts[j])
                    S.op("dve", lambda e, j=j: e.tensor_tensor(out=kts[j][:, :], in0=kts[j][:, :], in1=kt[j][:, :], op=ALU.add), r=[kts[j], kt[j]], w=[kts[j]])
                    S.op("dve", lambda e, j=j: e.scalar_tensor_tensor(out=sqb[:, :], in0=rkv[0][j][:, :], scalar=cv[:, j, CV["rk"]:CV["rk"] + 1], in1=kts[j][:, :], op0=ALU.mult, op1=ALU.mult), r=[rkv[0][j], cv, kts[j]], w=[sqb])
                    S.op("pe", lambda e: e.matmul(ps_prep[0][:, :], lhsT=bones[:, :], rhs=sqb[:, :], start=True, stop=True), r=[bones, sqb], w=[ps_prep[0]])
                    S.op("dve", lambda e, j=j: e.tensor_tensor(out=bon[j][:, :], in0=ps_prep[0][:, :], in1=rkv[2][j][:, :], op=ALU.mult), r=[ps_prep[0], rkv[2][j]], w=[bon[j]])
            if d == 1:
                S.op("act", lambda e: e.activation(out=xb[1][:, :], in_=xs[1][:, :], func=AF.Sigmoid), r=[xs[1]], w=[xb[1]])
                for j in range(NJ):
                    S.op("pe", lambda e, j=j: e.matmul(ps_prep[0][:, :], lhsT=g2_sb[:, j * 128:(j + 1) * 128], rhs=xb[1][:, :], start=True, stop=True), r=[g2_sb, xb[1]], w=[ps_prep[0]])
                    S.op("act", lambda e, j=j: e.copy(out=g_t[j][:, :], in_=ps_prep[0][:, :]), r=[ps_prep[0]], w=[g_t[j]])

            if d == 0 and sb == 0:
                for n in ("At", "Bt", "Bpt", "Kt", "Kpt", "Rt", "Vt"):
                    dbg(nc, S, n, fm[n][0], lambda n=n: fm[n][0].t[:, :], [128, SBK], BF16)
                dbg(nc, S, "cp", cp[0], lambda: cp[0].t[:, :], [128, SBK])
                dbg(nc, S, "lw", lw[0], lambda: lw[0].t[:, :], [128, SBK])
                dbg(nc, S, "kk", kkt[0], lambda: kkt[0].t[:, :], [128, SBK])
                dbg(nc, S, "r", rkv[0][0], lambda: rkv[0][0].t[:, :], [128, SBK])
                dbg(nc, S, "a", a_t[0][0], lambda: a_t[0][0].t[:, :], [128, SBK])
            if os.environ.get('BARRIER'):
                S.barrier()
            if STAGE < 2:
                continue
            if os.environ.get('UNITSB') is not None and str(sb) not in os.environ['UNITSB']:
                continue
            ch_order = list(range(NCH)) if d == 0 else list(range(NCH - 1, -1, -1))
            for c in ch_order:
                cs = slice(c * C, (c + 1) * C)
                gch = sb * NCH + c
                for j in range(NJ):
                    for e in range(2):
                        p0 = 0
                        P = slice(0, 64)
                        h = j * 2 + e
                        At, Bt, Bpt, Kt, Kpt, Rt, Vt = ((fm[n][j] if e == 0 else fmo[n][j]) for n in ("At", "Bt", "Bpt", "Kt", "Kpt", "Rt", "Vt"))
                        PCh = PCc[j] if e == 0 else PCo[j]
                        fmr = [At, Bt, Bpt, Kt, Kpt, Rt, Vt]
                        S.op("pe", lambda e_, P=P, e=e: e_.matmul(pa(PS1[e], lo=0, hi=128), lhsT=Bt.t[P, cs], rhs=At.t[P, cs], start=True, stop=True), r=[Bt, At], w=[PS1[e][3]])
                        S.op("pe", lambda e_, P=P, e=e: e_.matmul(pa(PS1[e], lo=128, hi=256), lhsT=Bt.t[P, cs], rhs=Rt.t[P, cs], start=True, stop=True), r=[Bt, Rt], w=[PS1[e][3]])
                        S.op("pe", lambda e_, P=P, e=e: e_.matmul(pa(PS2[e], lo=0, hi=128), lhsT=Kt.t[P, cs], rhs=At.t[P, cs], start=True, stop=True), r=[Kt, At], w=[PS2[e][3]])
                        S.op("pe", lambda e_, P=P, e=e: e_.matmul(pa(PS2[e], lo=128, hi=256), lhsT=Kt.t[P, cs], rhs=Rt.t[P, cs], start=True, stop=True), r=[Kt, Rt], w=[PS2[e][3]])
                        S.op("pe", lambda e_, P=P, e=e: e_.matmul(pa(PS3[e]), lhsT=At.t[P, cs], rhs=Bt.t[P, cs], start=True, stop=True), r=[At, Bt], w=[PS3[e][3]])
                        ps4 = PS4[e][0].t[:, PS4[e][1]:PS4[e][2]].bitcast(BF16)
                        for qi, src in enumerate((At, Bpt, Kpt, Vt)):
                            S.op("pe", lambda e_, P=P, qi=qi, src=src, ps4=ps4: e_.transpose(out=ps4[:, qi * 64:(qi + 1) * 64], in_=src.t[P, cs], identity=ident.t[P, p0:p0 + 64]), r=[src, ident], w=[PS4[e][3]])
                        if STAGE < 3:
                            continue
                        if 'a' in SUB: S.op("dve", lambda e_, e=e: e_.tensor_tensor(out=X1[e][:, :], in0=pa(PS1[e]), in1=mk2[d][:, :], op=ALU.mult), r=[PS1[e][3], mk2[d]], w=[X1[e]])
                        if 'a' in SUB: S.op("dve", lambda e_, e=e: e_.tensor_tensor(out=X2[e][:, :], in0=pa(PS2[e]), in1=mk2[d][:, :], op=ALU.mult), r=[PS2[e][3], mk2[d]], w=[X2[e]])
                        if 'a' in SUB: S.op("dve", lambda e_, e=e: e_.tensor_tensor(out=Lm[e][0][:, :], in0=pa(PS3[e]), in1=mkL[d][:, :], op=ALU.mult), r=[PS3[e][3], mkL[d]], w=[Lm[e][0]])
                        if 'b' in SUB: S.op("dve", lambda e_, e=e, ps4=ps4: e_.tensor_copy(out=TM[e][:, :], in_=ps4), r=[PS4[e][3]], w=[TM[e]])
                        if d == 0 and sb == 0 and c == 0 and j == 0 and e == 0:
                            dbg(nc, S, "X1", X1[0], lambda: X1[0].t[:, :], [128, 256], BF16)
                            dbg(nc, S, "X2", X2[0], lambda: X2[0].t[:, :], [128, 256], BF16)
                            dbg(nc, S, "L0", Lm[0][0], lambda: Lm[0][0].t[:, :], [128, 128], BF16)
                            dbg(nc, S, "TM", TM[0], lambda: TM[0].t[:, :], [128, 256], BF16)
                        if STAGE < 4:
                            continue
                        S.op("pe", lambda e_, e=e: e_.matmul(pa(PS5), lhsT=X2[e][:, 0:128], rhs=TM[e][:, 192:256], start=True, stop=True), r=[X2[e], TM[e]], w=[PS5[3]])
                        S.op("act", lambda e_, e=e: e_.copy(out=Z[e][0][:, 0:64], in_=TM[e][:, 0:64]), r=[TM[e]], w=[Z[e][0]])
                        S.op("dve", lambda e_, e=e: e_.tensor_copy(out=Z[e][0][:, 64:128], in_=pa(PS5)), r=[PS5[3]], w=[Z[e][0]])
                        Lsrc = Lm[e][0]
                        for lv in range(7):
                            S.op("pool", lambda e_, lv=lv, e=e, Lsrc=Lsrc: e_.tensor_tensor(out=Lmk[e][lv][:, :], in0=Lsrc.t[:, :], in1=lvm[:, lv, d, :], op=ALU.mult), r=[Lsrc, lvm], w=[Lmk[e][lv]])
                        S.op("pool", lambda e_, e=e: e_.tensor_tensor(out=XT[e][:, :], in0=X1[e][:, 0:128], in1=lvm[:, 0, 1 - d, :], op=ALU.mult), r=[X1[e], lvm], w=[XT[e]])
                        S.op("dve", lambda e_, e=e: e_.tensor_tensor(out=Tbuf[e][0][:, :], in0=Lmk[e][0][:, :], in1=ident[:, :], op=ALU.add), r=[Lmk[e][0], ident], w=[Tbuf[e][0]])
                        S.op("dve", lambda e_, e=e: e_.tensor_tensor(out=TTbuf[e][0][:, :], in0=XT[e][:, :], in1=ident[:, :], op=ALU.add), r=[XT[e], ident], w=[TTbuf[e][0]])
                        Tb = Tbuf[e][0]
                        TbT = TTbuf[e][0]
                        for lv in range(1, 7):
                            S.op("pe", lambda e_, lv=lv, e=e, TbT=TbT: e_.matmul(pa(PSn), lhsT=Lmk[e][lv][:, :], rhs=TbT.t[:, :], start=True, stop=True), r=[Lmk[e][lv], TbT], w=[PSn[3]])
                            S.op("act", lambda e_, e=e: e_.copy(out=XT[e][:, :], in_=pa(PSn)), r=[PSn[3]], w=[XT[e]])
                            S.op("pe", lambda e_, e=e, Tb=Tb: e_.matmul(pa(PSz), lhsT=XT[e][:, :], rhs=Tb.t[:, :], start=True, stop=True), r=[XT[e], Tb], w=[PSz[3]])
                            S.op("pe", lambda e_, e=e, Tb=Tb: e_.matmul(pa(PSl), lhsT=Tb.t[:, :], rhs=XT[e][:, :], start=True, stop=True), r=[XT[e], Tb], w=[PSl[3]])
                            Tn = Tbuf[e][lv % 2]
                            TnT = TTbuf[e][lv % 2]
                            S.op("dve", lambda e_, Tn=Tn, Tb=Tb: e_.tensor_tensor(out=Tn[:, :], in0=pa(PSz), in1=Tb.t[:, :], op=ALU.add), r=[PSz[3], Tb], w=[Tn])
                            S.op("dve", lambda e_, TnT=TnT, TbT=TbT: e_.tensor_tensor(out=TnT[:, :], in0=pa(PSl), in1=TbT.t[:, :], op=ALU.add), r=[PSl[3], TbT], w=[TnT])
                            Tb, TbT = Tn, TnT
                        S.op("pe", lambda e_, e=e, TbT=TbT: e_.matmul(pa(PSz), lhsT=TbT.t[:, :], rhs=Z[e][0][:, :], start=True, stop=True), r=[TbT, Z[e][0]], w=[PSz[3]])
                        S.op("dve", lambda e_, e=e: e_.tensor_copy(out=Z[e][1][:, :], in_=pa(PSz)), r=[PSz[3]], w=[Z[e][1]])
                        zc = 1
                        Zf = Z[e][zc]
                        if d == 0 and sb == 0 and c == 0 and j == 0 and e == 0:
                            dbg(nc, S, "Zf", Zf, lambda Zf=Zf: Zf.t[:, :], [128, 128], BF16)
                        if STAGE < 5:
                            continue
                        if 'q' in SUB5: S.op("pe", lambda e_, Zf=Zf, e=e, P=P: e_.matmul(pa(PSq), lhsT=Zf.t[:, 0:128], rhs=X1[e][:, 128:256], start=True, stop=True), r=[Zf, X1[e]], w=[PSq[3]])
                        if 'q' in SUB5: S.op("dve", lambda e_, e=e, P=P: e_.tensor_tensor(out=Qt[e][P, :], in0=pa(PSq, p0, p0 + 64), in1=Rt.t[P, cs], op=ALU.add), r=[PSq[3], Rt], w=[Qt[e]])
                        li = c * C + (C - 1 if d == 0 else 0)
                        if 'g' in SUB5: S.op("pe", lambda e_, Zf=Zf, e=e: e_.matmul(pa(PSg), lhsT=Zf.t[:, 0:128], rhs=TM[e][:, 64:128], start=True, stop=True), r=[Zf, TM[e]], w=[PSg[3]])
                        if 'g' in SUB5: S.op("dve", lambda e_, e=e, P=P, li=li: e_.scalar_tensor_tensor(out=Gt[e][P, 0:64], in0=identf.t[P, p0:p0 + 64], scalar=PCh.t[P, c:c + 1], in1=pa(PSg, p0, p0 + 64), op0=ALU.mult, op1=ALU.add), r=[identf, PCh, PSg[3]], w=[Gt[e]])
                        if 'h' in SUB5: S.op("pe", lambda e_, Zf=Zf, e=e: e_.matmul(pa(PSh), lhsT=TM[e][:, 64:192], rhs=Zf.t[:, 64:128], start=True, stop=False), r=[Zf, TM[e]], w=[PSh[3]])
                        if 'h' in SUB5: S.op("pe", lambda e_, e=e: e_.matmul(pa(PSh), lhsT=TM[e][:, 128:256], rhs=TM[e][:, 192:256], start=False, stop=True), r=[TM[e]], w=[PSh[3]])
                        if 'h' in SUB5: S.op("act", lambda e_, e=e, P=P: e_.copy(out=Hs[e][P, :], in_=pa(PSh, p0, p0 + 64)), r=[PSh[3]], w=[Hs[e]])
                        if d == 0 and sb == 0 and c == 0 and j == 0 and e == 0:
                            dbg(nc, S, "Qt", Qt[0], lambda: Qt[0].t[0:64, :], [64, 128], BF16)
                            dbg(nc, S, "Gt", Gt[0], lambda: Gt[0].t[0:64, :], [64, 64], BF16)
                            dbg(nc, S, "Hs", Hs[0], lambda: Hs[0].t[0:64, :], [64, 64])
                        if STAGE < 6:
                            continue
                        ci = cur[j][e]
                        Scur = Sth[j][ci][e]
                        S.op("pe", lambda e_, Zf=Zf, e=e: e_.matmul(pa(PSY[e]), lhsT=X1[e][:, 128:256], rhs=Zf.t[:, 64:128], start=True, stop=False), r=[X1[e], Zf], w=[PSY[e][3]])
                        S.op("pe", lambda e_, e=e: e_.matmul(pa(PSY[e]), lhsT=X2[e][:, 128:256], rhs=TM[e][:, 192:256], start=False, stop=False), r=[X2[e], TM[e]], w=[PSY[e][3]])
                        if 'b' in SUB6: S.op("pe", lambda e_, e=e, P=P, Scur=Scur: e_.matmul(pa(PSY[e]), lhsT=Qt[e][:, :], rhs=Scur.t[:, :], start=False, stop=True), r=[Qt[e], Scur], w=[PSY[e][3]])
                        S.op("dve", lambda e_, e=e, h=h: e_.tensor_copy(out=ytok[:, h * 64:(h + 1) * 64], in_=pa(PSY[e])), r=[PSY[e][3]], w=[ytok_reg[h]])
                        if STAGE < 7:
                            continue
                        Snew = Sth[j][1 - ci][e]
                        S.op("pe", lambda e_, e=e, P=P, Scur=Scur: e_.matmul(pa(PSs[e]), lhsT=Gt[e][:, :], rhs=Scur.t[:, :], start=True, stop=True), r=[Gt[e], Scur], w=[PSs[e][3]])
                        S.op("dve", lambda e_, e=e, P=P, Snew=Snew: e_.tensor_tensor(out=Snew.t[P, :], in0=pa(PSs[e], p0, p0 + 64), in1=Hs[e][P, :], op=ALU.add), r=[PSs[e][3], Hs[e]], w=[Snew])
                        cur[j][e] = 1 - ci
                        if os.environ.get('UBAR'):
                            S.barrier()
                if d == 0 and sb == 0 and c == 0:
                    dbg(nc, S, "ytok", ytok_reg, lambda: ytok.t[:, :], [128, 256])
                if STAGE < 8:
                    continue
                if d == 0:
                    S.dma("sp", lambda e_, gch=gch: e_.dma_start(out=yacc_d[gch, :, :], in_=ytok[:, :]), r=ytok_reg, w=[yacc_reg[gch]], final=(1 not in dirs))
                else:
                    S.dma("sp", lambda e_, gch=gch: e_.dma_start(out=yprev[:, :], in_=yacc_d[gch, :, :]), r=[yacc_reg[gch]], w=[yprev])
                    S.op("dve", lambda e_: e_.tensor_tensor(out=yn[:, :], in0=ytok[:, :], in1=yprev[:, :], op=ALU.add), r=ytok_reg + [yprev], w=[yn])
                    for h in range(NH):
                        S.op("dve", lambda e_, h=h: e_.bn_stats(out=stats[:, h, :], in_=yn[:, h * 64:(h + 1) * 64]), r=[yn], w=[stats])
                    for h in range(NH):
                        S.op("dve", lambda e_, h=h: e_.bn_aggr(out=mv[:, h, :], in_=stats[:, h, :]), r=[stats], w=[mv])
                    S.op("act", lambda e_: e_.activation(out=rstd[:, :], in_=mv[:, :, 1], func=AF.Ln, bias=epsgn[:, 0:1]), r=[mv, epsgn], w=[rstd])
                    S.op("act", lambda e_: e_.activation(out=rstd[:, :], in_=rstd[:, :], func=AF.Exp, scale=-0.5), r=[rstd], w=[rstd])
                    for h in range(NH):
                        S.op("dve", lambda e_, h=h: e_.tensor_scalar(out=yn[:, h * 64:(h + 1) * 64], in0=yn[:, h * 64:(h + 1) * 64], scalar1=mv[:, h, 0:1], scalar2=rstd[:, h:h + 1], op0=ALU.subtract, op1=ALU.mult), r=[yn, mv, rstd], w=[yn])
                    S.op("dve", lambda e_: e_.tensor_tensor(out=yn[:, :], in0=yn[:, :], in1=gnwb_sb[:, 0, :], op=ALU.mult), r=[yn, gnwb_sb], w=[yn])
                    S.op("dve", lambda e_: e_.tensor_tensor(out=yn[:, :], in0=yn[:, :], in1=gnwb_sb[:, 1, :], op=ALU.add), r=[yn, gnwb_sb], w=[yn])
                    for j in range(NJ):
                        S.op("pe", lambda e_, j=j: e_.transpose(out=pa(PStr), in_=yn[:, j * 128:(j + 1) * 128], identity=identf[:, :]), r=[yn, identf], w=[PStr[3]])
                        S.op("dve", lambda e_, j=j: e_.tensor_tensor(out=yo[j][:, cs], in0=pa(PStr), in1=bon[j][:, cs], op=ALU.add), r=[PStr[3], bon[j]], w=[yo[j]])
                        S.op("dve", lambda e_, j=j: e_.tensor_tensor(out=yo[j][:, cs], in0=yo[j][:, cs], in1=g_t[j][:, cs], op=ALU.mult), r=[yo[j], g_t[j]], w=[yo[j]])
            if d == 1:
                for j in range(NJ):
                    S.dma("sp", lambda e_, j=j: e_.dma_start(out=d_out[j * 128:(j + 1) * 128, t0:t0 + SBK], in_=yo[j][:, :]), r=[yo[j]], final=True)


TQ = 2048
TK = 4096
HALO = 1024
BR = (1, 4, 16)
A_NORM_EPS = 1e-6


def key_tiles():
    out = []
    for d in BR:
        nb = TQ // (128 * d)
        for r in range(d):
            for i in range(nb + 1):
                out.append((d, r, i, HALO // d - 64 + 128 * i))
    return out


KT = key_tiles()
NKT = len(KT)


def emit_attn(nc, S, d_in, d_out, consts):
    qT = d_in["qT"]
    kT = d_in["kT"]
    vT = d_in["vT"]
    vld = d_in["vld"]
    qkg = d_in["qkg"]
    ident = consts["ident"]

    ones_f = S.sb([128, 512], F32, "a_ones")
    S.op("pool", lambda e: e.memset(ones_f[:, :], 1.0), w=[ones_f])
    bones = S.sb([128, 128], BF16, "a_bones")
    S.op("pool", lambda e: e.memset(bones[:, :], 0.0), w=[bones])
    S.op("pool", lambda e: e.memset(bones[0:64, 0:64], 1.0), w=[bones])
    S.op("pool", lambda e: e.memset(bones[64:128, 64:128], 1.0), w=[bones])
    vld_sb = S.sb([128, NKT], F32, "vld_sb")
    S.dma("sp", lambda e: e.dma_start(out=vld_sb[:, :], in_=vld[:, :]), w=[vld_sb])
    qkg_sb = S.sb([128, 2], F32, "qkg_sb")
    S.dma("sp", lambda e: e.dma_start(out=qkg_sb[:, :], in_=qkg[:, :]), w=[qkg_sb])
    qsc = S.sb([128, 1], F32, "qsc")
    S.op("dve", lambda e: e.tensor_scalar(out=qsc[:, :], in0=qkg_sb[:, 0:1], scalar1=0.125, scalar2=None, op0=ALU.mult), r=[qkg_sb], w=[qsc])
    epsb = S.sb([128, 1], F32, "a_eps")
    S.op("pool", lambda e: e.memset(epsb[:, :], A_NORM_EPS), w=[epsb])

    reli = S.sb([128, 128], I32, "reli")
    relf = S.sb([128, 128], F32, "relf")
    band = S.sb([128, 128], F32, "band")
    mtmp = S.sb([128, 128], F32, "mtmp")
    masks = {}
    for d in BR:
        for ab, base in (("A", -64), ("B", 64)):
            m = S.sb([128, 8, 128], BF16, "mask_%d%s" % (d, ab))
            S.op("pool", lambda e, base=base: e.iota(reli[:, :], pattern=[[-1, 128]], base=base, channel_multiplier=1), w=[reli])
            S.op("dve", lambda e: e.tensor_copy(out=relf[:, :], in_=reli[:, :]), r=[reli], w=[relf])
            S.op("act", lambda e: e.activation(out=relf[:, :], in_=relf[:, :], func=AF.Abs), r=[relf], w=[relf])
            S.op("dve", lambda e: e.tensor_single_scalar(out=band[:, :], in_=relf[:, :], scalar=64.5, op=ALU.is_le), r=[relf], w=[band])
            for h in range(8):
                slope = 2.0 ** (-(h + 1))
                S.op("act", lambda e, slope=slope, d=d: e.activation(out=mtmp[:, :], in_=relf[:, :], func=AF.Exp, scale=-slope * d), r=[relf], w=[mtmp])
                S.op("dve", lambda e, m=m, h=h: e.tensor_tensor(out=m[:, h, :], in0=mtmp[:, :], in1=band[:, :], op=ALU.mult), r=[mtmp, band], w=[m])
            masks[(d, ab)] = m

    stage = S.sb([128, TK], F32, "stage")
    sqb = S.sb([128, 512], BF16, "a_sqb")
    rs = S.sb([128, 512], F32, "a_rs")
    Qb = [S.sb([128, 2, TQ], BF16, "Qb%d" % i) for i in range(2)]
    Kb = [S.sb([128, TK], BF16, "Kb%d" % i) for i in range(2)]
    Vb = [S.sb([128, TK], BF16, "Vb%d" % i) for i in range(2)]
    acc = S.sb([128, 4, TQ], F32, "acc")
    acc_reg = [[Reg("acc%d_%d" % (h, qb)) for qb in range(TQ // 128)] for h in range(4)]
    NVX = 17
    Vx = [S.sb([128, 4, 128], BF16, "Vx%d" % i) for i in range(NVX)]
    ex = [S.sb([128, 256], F32, "a_ex%d" % i) for i in range(2)]
    Pt = [S.sb([128, 256], BF16, "a_Pt%d" % i) for i in range(4)]
    rec = S.sb([64, TQ], F32, "rec")
    o_sb = S.sb([64, TQ], F32, "o_sb")
    ps_n = S.ps([128, 512], F32, "a_ps_n")
    ps_s = [S.ps([128, 512], F32, "a_ps_s%d" % i) for i in range(2)]
    ps_t = S.ps([128, 512], F32, "a_ps_t")
    ps_o = [S.ps([128, 512], F32, "a_ps_o%d" % i) for i in range(2)]
    po_reg = [[Reg("po%d_%d" % (i, k)) for k in range(2)] for i in range(2)]
    for i in range(2):
        S.op("pool", lambda e, i=i: e.memset(Qb[i][:, :, :], 0.0), w=[Qb[i]])

    def qknorm(src_rows, ncols, gcol, dst_fn):
        S.dma("sp", lambda e: e.dma_start(out=stage[:, 0:ncols], in_=src_rows), w=[stage])
        for c0 in range(0, ncols, 512):
            S.op("act", lambda e, c0=c0: e.activation(out=sqb[:, :], in_=stage[:, c0:c0 + 512], func=AF.Square), r=[stage], w=[sqb])
            S.op("pe", lambda e: e.matmul(ps_n[:, :], lhsT=bones[:, :], rhs=sqb[:, :], start=True, stop=True), r=[bones, sqb], w=[ps_n])
            S.op("act", lambda e: e.activation(out=rs[:, :], in_=ps_n[:, :], func=AF.Ln, scale=1.0 / 64, bias=epsb[:, 0:1]), r=[ps_n, epsb], w=[rs])
            S.op("act", lambda e: e.activation(out=rs[:, :], in_=rs[:, :], func=AF.Exp, scale=-0.5), r=[rs], w=[rs])
            for out_ap, ps in dst_fn(c0):
                S.op("dve", lambda e, out_ap=out_ap, ps=ps, c0=c0: e.scalar_tensor_tensor(out=out_ap, in0=stage[ps, c0:c0 + 512], scalar=gcol[ps, 0:1], in1=rs[ps, :], op0=ALU.mult, op1=ALU.mult), r=[stage, rs, qsc, qkg_sb], w=[dst_fn.tile])

    for grp in range(2):
        for pi in range(2):
            pair = grp * 2 + pi
            rows = slice(pair * 128, (pair + 1) * 128)

            def dq(c0, pi=pi):
                return [(Qb[pi].t[0:64, 0, c0:c0 + 512], slice(0, 64)), (Qb[pi].t[64:128, 1, c0:c0 + 512], slice(64, 128))]
            dq.tile = Qb[pi]
            qknorm(qT[rows, :], TQ, qsc, dq)

            def dk(c0, pi=pi):
                return [(Kb[pi].t[:, c0:c0 + 512], slice(0, 128))]
            dk.tile = Kb[pi]
            qknorm(kT[rows, :], TK, qkg_sb.t[:, 1:2], dk)
            S.dma("sp", lambda e, rows=rows: e.dma_start(out=stage[:, :], in_=vT[rows, :]), w=[stage])
            S.op("act", lambda e, pi=pi: e.copy(out=Vb[pi][:, :], in_=stage[:, :]), r=[stage], w=[Vb[pi]])
        kt_idx = 0
        first_branch = True
        pti = 0
        for d in BR:
            nb = TQ // (128 * d)
            for r in range(d):
                for i in range(nb + 1):
                    m0 = HALO // d - 64 + 128 * i
                    p0 = r + d * m0
                    psl = slice(p0, p0 + 127 * d + 1, d)
                    pst = ps_t.t[:, 0:128].bitcast(BF16)
                    for pi in range(2):
                        S.op("pe", lambda e, pi=pi, psl=psl, pst=pst: e.transpose(out=pst[:, pi * 128:(pi + 1) * 128], in_=Vb[pi].t[:, psl], identity=ident[:, :]), r=[Vb[pi], ident], w=[ps_t])
                    S.op("dve", lambda e, i=i, pst=pst: e.tensor_copy(out=Vx[i].t[:, :, 0:64], in_=pst.rearrange("p (h x) -> p h x", x=64)), r=[ps_t], w=[Vx[i]])
                    S.op("pool", lambda e, i=i, k=kt_idx + i: e.tensor_scalar(out=Vx[i].t[:, :, 64:128], in0=ones_f.t[:, 0:256].rearrange("p (h x) -> p h x", x=64), scalar1=vld_sb[:, k:k + 1], scalar2=None, op0=ALU.mult), r=[ones_f, vld_sb], w=[Vx[i]])
                for J in range(nb):
                    mq0 = HALO // d + 128 * J
                    q0 = r + d * mq0 - HALO
                    qsl = slice(q0, q0 + 127 * d + 1, d)
                    qb0 = q0 // 128
                    pts = []
                    for ti, ab in ((J, "A"), (J + 1, "B")):
                        m0 = HALO // d - 64 + 128 * ti
                        p0 = r + d * m0
                        psl = slice(p0, p0 + 127 * d + 1, d)
                        for pi in range(2):
                            pss = ps_s[pi]
                            S.op("pe", lambda e, pi=pi, psl=psl, qsl=qsl, pss=pss: e.matmul(pss.t[:, 0:256].rearrange("p (a b) -> p a b", a=2), lhsT=Kb[pi].t[:, psl], rhs=Qb[pi].t[:, :, qsl], start=True, stop=True), r=[Kb[pi], Qb[pi]], w=[pss])
                            exi = ex[pi]
                            S.op("act", lambda e, pss=pss, exi=exi: e.activation(out=exi[:, :], in_=pss.t[:, 0:256], func=AF.Exp), r=[pss], w=[exi])
                            pt = Pt[pti % 4]
                            pti += 1
                            pair = grp * 2 + pi
                            S.op("dve", lambda e, pt=pt, exi=exi, d=d, ab=ab, pair=pair: e.tensor_tensor(out=pt.t[:, :].rearrange("p (a b) -> p a b", a=2), in0=exi.t[:, :].rearrange("p (a b) -> p a b", a=2), in1=masks[(d, ab)].t[:, pair * 2:pair * 2 + 2, :], op=ALU.mult), r=[exi, masks[(d, ab)]], w=[pt])
                            pts.append((ti, pi, pt))
                    for hh in range(4):
                        pi, hp = hh // 2, hh % 2
                        po = ps_o[hh // 2]
                        col = (hh % 2) * 128
                        mm = [(ti, pt) for (ti, ppi, pt) in pts if ppi == pi]
                        for n_, (ti, pt) in enumerate(mm):
                            S.op("pe", lambda e, ti=ti, pt=pt, hh=hh, hp=hp, po=po, col=col, n_=n_: e.matmul(po.t[:, col:col + 128], lhsT=Vx[ti].t[:, hh, :], rhs=pt.t[:, hp * 128:(hp + 1) * 128], start=(n_ == 0), stop=(n_ == len(mm) - 1)), r=[Vx[ti], pt], w=[po_reg[hh // 2][hh % 2]])
                        areg = acc_reg[hh][qb0] if d == 1 else [acc_reg[hh][q] for q in range(TQ // 128)]
                        if first_branch:
                            S.op("dve", lambda e, hh=hh, po=po, col=col, qsl=qsl: e.tensor_copy(out=acc.t[:, hh, qsl], in_=po.t[:, col:col + 128]), r=[po_reg[hh // 2][hh % 2]], w=[areg])
                        else:
                            S.op("dve", lambda e, hh=hh, po=po, col=col, qsl=qsl: e.tensor_tensor(out=acc.t[:, hh, qsl], in0=po.t[:, col:col + 128], in1=acc.t[:, hh, qsl], op=ALU.add), r=[po_reg[hh // 2][hh % 2], areg], w=[areg])
                kt_idx += nb + 1
            first_branch = False
        allreg = lambda hh: [acc_reg[hh][q] for q in range(TQ // 128)]
        for hh in range(4):
            S.op("dve", lambda e, hh=hh: e.reciprocal(out=rec[:, :], in_=acc.t[64:128, hh, :]), r=allreg(hh), w=[rec])
            S.op("dve", lambda e, hh=hh: e.tensor_tensor(out=o_sb[:, :], in0=acc.t[0:64, hh, :], in1=rec[:, :], op=ALU.mult), r=allreg(hh) + [rec], w=[o_sb])
            h = grp * 4 + hh
            S.dma("sp", lambda e, h=h: e.dma_start(out=d_out[h * 64:(h + 1) * 64, :], in_=o_sb[:, :]), r=[o_sb], final=True)


D = 1024
DFF = 4096
NPROJ = 3328
TOK = 2048
TB = 512
NBLK = TOK // TB
T_NORM_EPS = 1e-6


def emit_tok(nc, S, d_in, d_out, do_mix, do_proj):
    xT_d = d_in["xT"]
    x = S.sb([128, 8, TOK], F32, "x_sb")
    xreg = [Reg("x_blk%d" % b) for b in range(NBLK)]
    hb = S.sb([128, 8, TOK], BF16, "hb")
    hreg = [Reg("hb%d" % b) for b in range(NBLK)]
    ones = S.sb([128, 128], BF16, "t_ones")
    epsb = S.sb([128, 1], F32, "t_eps")
    sq = S.sb([128, 8, TB], BF16, "t_sq")
    rstd = S.sb([128, TB], F32, "t_rstd")
    gv = S.sb([128, 2, 8], F32, "gv_sb")
    wA = [S.sb([128, 8, 512], BF16, "wA%d" % i) for i in range(2)]
    wB = [S.sb([128, 4, 1024], BF16, "wB%d" % i) for i in range(2)]
    hid = [S.sb([128, 4, TB], BF16, "hid%d" % i) for i in range(2)]
    rl = [S.sb([128, TB], F32, "t_rl%d" % i) for i in range(2)]
    stg = [S.sb([128, TB], F32, "stg%d" % i) for i in range(2)]
    ps_ss = S.ps([128, TB], F32, "t_ps_ss")
    ps_a = [S.ps([128, TB], F32, "t_ps_a%d" % i) for i in range(3)]
    ps_b = [S.ps([128, TB], F32, "t_ps_b%d" % i) for i in range(3)]

    for b in range(NBLK):
        S.dma("sp", lambda e, b=b: e.dma_start(out=x[:, :, b * TB:(b + 1) * TB], in_=xT_d.rearrange("(c p) t -> p c t", p=128)[:, :, b * TB:(b + 1) * TB]), w=[xreg[b]])
    S.dma("sp", lambda e: e.dma_start(out=gv[:, :, :], in_=d_in["gv"][:, :, :]), w=[gv])
    S.op("pool", lambda e: e.memset(ones[:, :], 1.0), w=[ones])
    S.op("pool", lambda e: e.memset(epsb[:, :], T_NORM_EPS), w=[epsb])
    cnt = {"a": 0, "b": 0, "pa": 0, "pb": 0, "rl": 0, "hid": 0, "stg": 0}

    def rmsnorm(which):
        for b in range(NBLK):
            ts = slice(b * TB, (b + 1) * TB)
            for c in range(8):
                S.op("act", lambda e, c=c: e.activation(out=sq[:, c, :], in_=x[:, c, ts], func=AF.Square), r=[xreg[b]], w=[sq])
            for c in range(8):
                S.op("pe", lambda e, c=c: e.matmul(ps_ss[:, :], lhsT=ones[:, :], rhs=sq[:, c, :], start=(c == 0), stop=(c == 7)), r=[ones, sq], w=[ps_ss])
            S.op("act", lambda e: e.activation(out=rstd[:, :], in_=ps_ss[:, :], func=AF.Ln, scale=1.0 / D, bias=epsb[:, 0:1]), r=[ps_ss, epsb], w=[rstd])
            S.op("act", lambda e: e.activation(out=rstd[:, :], in_=rstd[:, :], func=AF.Exp, scale=-0.5), r=[rstd], w=[rstd])
            for c in range(8):
                S.op("dve", lambda e, c=c: e.scalar_tensor_tensor(out=hb[:, c, ts], in0=x[:, c, ts], scalar=gv[:, which, c:c + 1], in1=rstd[:, :], op0=ALU.mult, op1=ALU.mult), r=[xreg[b], gv, rstd], w=[hreg[b]])

    if do_mix:
        catT = d_in["catT"]
        for b in range(NBLK):
            ts = slice(b * TB, (b + 1) * TB)
            for c in range(8):
                st = stg[cnt["stg"] % 2]
                cnt["stg"] += 1
                S.dma("sp", lambda e, st=st, c=c: e.dma_start(out=st[:, :], in_=catT[c * 128:(c + 1) * 128, ts]), w=[st])
                S.op("act", lambda e, st=st, c=c: e.copy(out=hb[:, c, ts], in_=st[:, :]), r=[st], w=[hreg[b]])
        w_out = d_in["w_out"]
        for g in range(2):
            wt = wA[cnt["a"] % 2]
            cnt["a"] += 1
            S.dma("pool", lambda e, wt=wt, g=g: e.dma_start(out=wt[:, :, :], in_=w_out.rearrange("(c p) n -> p c n", p=128)[:, :, g * 512:(g + 1) * 512]), w=[wt])
            for b in range(NBLK):
                ts = slice(b * TB, (b + 1) * TB)
                for mm in range(4):
                    m = g * 4 + mm
                    pd = ps_a[cnt["pa"] % 3]
                    cnt["pa"] += 1
                    for c in range(8):
                        S.op("pe", lambda e, c=c, mm=mm, wt=wt, pd=pd: e.matmul(pd[:, :], lhsT=wt[:, c, mm * 128:(mm + 1) * 128], rhs=hb[:, c, ts], start=(c == 0), stop=(c == 7)), r=[wt, hreg[b]], w=[pd])
                    S.op("dve", lambda e, m=m, pd=pd: e.tensor_tensor(out=x[:, m, ts], in0=pd[:, :], in1=x[:, m, ts], op=ALU.add), r=[pd, xreg[b]], w=[xreg[b]])
        rmsnorm(0)
        w_up = d_in["w_up"]
        w_down = d_in["w_down"]
        for fg in range(8):
            wu = wA[cnt["a"] % 2]
            cnt["a"] += 1
            wd = wB[cnt["b"] % 2]
            cnt["b"] += 1
            S.dma("pool", lambda e, wu=wu, fg=fg: e.dma_start(out=wu[:, :, :], in_=w_up.rearrange("(c p) n -> p c n", p=128)[:, :, fg * 512:(fg + 1) * 512]), w=[wu])
            S.dma("pool", lambda e, wd=wd, fg=fg: e.dma_start(out=wd[:, :, :], in_=w_down[fg * 512:(fg + 1) * 512, :].rearrange("(n p) m -> p n m", p=128)), w=[wd])
            for b in range(NBLK):
                ts = slice(b * TB, (b + 1) * TB)
                hd = hid[cnt["hid"] % 2]
                cnt["hid"] += 1
                for nn in range(4):
                    pu = ps_a[cnt["pa"] % 3]
                    cnt["pa"] += 1
                    for c in range(8):
                        S.op("pe", lambda e, c=c, nn=nn, wu=wu, pu=pu: e.matmul(pu[:, :], lhsT=wu[:, c, nn * 128:(nn + 1) * 128], rhs=hb[:, c, ts], start=(c == 0), stop=(c == 7)), r=[wu, hreg[b]], w=[pu])
                    r_ = rl[cnt["rl"] % 2]
                    cnt["rl"] += 1
                    S.op("act", lambda e, pu=pu, r_=r_: e.activation(out=r_[:, :], in_=pu[:, :], func=AF.Relu), r=[pu], w=[r_])
                    S.op("pool", lambda e, nn=nn, r_=r_, hd=hd: e.tensor_tensor(out=hd[:, nn, :], in0=r_[:, :], in1=r_[:, :], op=ALU.mult), r=[r_], w=[hd])
                for m in range(8):
                    pd = ps_b[cnt["pb"] % 3]
                    cnt["pb"] += 1
                    for nn in range(4):
                        S.op("pe", lambda e, nn=nn, m=m, wd=wd, pd=pd, hd=hd: e.matmul(pd[:, :], lhsT=wd[:, nn, m * 128:(m + 1) * 128], rhs=hd[:, nn, :], start=(nn == 0), stop=(nn == 3)), r=[wd, hd], w=[pd])
                    S.op("dve", lambda e, m=m, pd=pd: e.tensor_tensor(out=x[:, m, ts], in0=pd[:, :], in1=x[:, m, ts], op=ALU.add), r=[pd, xreg[b]], w=[xreg[b]])
        xo = d_out["xT"]
        for b in range(NBLK):
            S.dma("sp", lambda e, b=b: e.dma_start(out=xo.rearrange("(c p) t -> p c t", p=128)[:, :, b * TB:(b + 1) * TB], in_=x[:, :, b * TB:(b + 1) * TB]), r=[xreg[b]], final=True)
    if do_proj:
        rmsnorm(1)
        w_in = d_in["w_in"]
        po = d_out["projT"]
        ngrp = (NPROJ + 511) // 512
        for g in range(ngrp):
            ncol = min(512, NPROJ - g * 512)
            wt = wA[cnt["a"] % 2]
            cnt["a"] += 1
            S.dma("pool", lambda e, wt=wt, g=g, ncol=ncol: e.dma_start(out=wt[:, :, 0:ncol], in_=w_in.rearrange("(c p) n -> p c n", p=128)[:, :, g * 512:g * 512 + ncol]), w=[wt])
            for b in range(NBLK):
                ts = slice(b * TB, (b + 1) * TB)
                for mm in range(ncol // 128):
                    n = g * 4 + mm
                    pd = ps_a[cnt["pa"] % 3]
                    cnt["pa"] += 1
                    for c in range(8):
                        S.op("pe", lambda e, c=c, mm=mm, wt=wt, pd=pd: e.matmul(pd[:, :], lhsT=wt[:, c, mm * 128:(mm + 1) * 128], rhs=hb[:, c, ts], start=(c == 0), stop=(c == 7)), r=[wt, hreg[b]], w=[pd])
                    st = stg[cnt["stg"] % 2]
                    cnt["stg"] += 1
                    if cnt["stg"] % 2:
                        S.op("act", lambda e, st=st, pd=pd: e.copy(out=st[:, :], in_=pd[:, :]), r=[pd], w=[st])
                    else:
                        S.op("dve", lambda e, st=st, pd=pd: e.tensor_copy(out=st[:, :], in_=pd[:, :]), r=[pd], w=[st])
                    S.dma("sp", lambda e, st=st, n=n: e.dma_start(out=po[n * 128:(n + 1) * 128, ts], in_=st[:, :]), r=[st], final=True)

from concourse.bass_utils import run_bass_kernel_spmd

DEPTH = 4
SEQ = 4096
NB = 4


def _consts(nc, S):
    ones = S.sb([128, 128], F32, "c_ones")
    S.op("pool", lambda e: e.memset(ones[:, :], 1.0), w=[ones])
    identf = S.sb([128, 128], F32, "identf")
    ident = S.sb([128, 128], BF16, "ident")
    S.op("pool", lambda e: e.affine_select(out=identf[:, :], in_=ones[:, :], pattern=[[1, 128]], compare_op=ALU.is_equal, fill=0.0, base=0, channel_multiplier=-1), r=[ones], w=[identf])
    S.op("dve", lambda e: e.tensor_copy(out=ident[:, :], in_=identf[:, :]), r=[identf], w=[ident])
    return dict(ident=ident, identf=identf)


def _level_masks():
    i = np.arange(128)[:, None]
    j = np.arange(128)[None, :]
    m = np.zeros((128, 7, 2, 128), np.float32)
    for lv in range(7):
        b = 1 << lv
        lo = ((i // (2 * b)) == (j // (2 * b))) & ((i % (2 * b)) >= b) & ((j % (2 * b)) < b)
        m[:, lv, 0, :] = lo
        m[:, lv, 1, :] = lo.T
    return m


def build_tok(do_mix, do_proj):
    nc = bass.Bass("TRN2", target_bir_lowering=False)
    d_in = dict(xT=nc.dram_tensor("xT", [D, TOK], F32, kind="ExternalInput").ap(),
                gv=nc.dram_tensor("gv", [128, 2, 8], F32, kind="ExternalInput").ap())
    d_out = {}
    if do_mix:
        d_in["catT"] = nc.dram_tensor("catT", [D, TOK], F32, kind="ExternalInput").ap()
        d_in["w_out"] = nc.dram_tensor("w_out", [D, D], F32, kind="ExternalInput").ap()
        d_in["w_up"] = nc.dram_tensor("w_up", [D, DFF], F32, kind="ExternalInput").ap()
        d_in["w_down"] = nc.dram_tensor("w_down", [DFF, D], F32, kind="ExternalInput").ap()
        d_out["xT"] = nc.dram_tensor("xT_out", [D, TOK], F32, kind="ExternalOutput").ap()
    if do_proj:
        d_in["w_in"] = nc.dram_tensor("w_in", [D, NPROJ], F32, kind="ExternalInput").ap()
        d_out["projT"] = nc.dram_tensor("projT", [NPROJ, TOK], F32, kind="ExternalOutput").ap()
    S = Sched(nc)
    emit_tok(nc, S, d_in, d_out, do_mix, do_proj)
    S.finish()
    return nc


def build_attn():
    nc = bass.Bass("TRN2", target_bir_lowering=False)
    d_in = dict(
        qT=nc.dram_tensor("qT", [512, TQ], F32, kind="ExternalInput").ap(),
        kT=nc.dram_tensor("kT", [512, TK], F32, kind="ExternalInput").ap(),
        vT=nc.dram_tensor("vT", [512, TK], F32, kind="ExternalInput").ap(),
        vld=nc.dram_tensor("vld", [128, NKT], F32, kind="ExternalInput").ap(),
        qkg=nc.dram_tensor("qkg", [128, 2], F32, kind="ExternalInput").ap(),
    )
    out = nc.dram_tensor("attnT", [512, TQ], F32, kind="ExternalOutput").ap()
    S = Sched(nc)
    consts = _consts(nc, S)
    emit_attn(nc, S, d_in, out, consts)
    S.finish()
    return nc


def build_rwkv(T):
    nc = bass.Bass("TRN2", target_bir_lowering=False)
    d_in = dict(
        u_rkv=nc.dram_tensor("u_rkv", [3, 256, T + 2], F32, kind="ExternalInput").ap(),
        u_x=nc.dram_tensor("u_x", [256, T + 2], F32, kind="ExternalInput").ap(),
        cvec=nc.dram_tensor("cvec", [128, 2, NCV], F32, kind="ExternalInput").ap(),
        xvec=nc.dram_tensor("xvec", [128, 2, NXV], F32, kind="ExternalInput").ap(),
        wa2=nc.dram_tensor("wa2", [128, 4, 256], F32, kind="ExternalInput").ap(),
        g2=nc.dram_tensor("g2", [128, 256], F32, kind="ExternalInput").ap(),
        gnwb=nc.dram_tensor("gnwb", [128, 2, 256], F32, kind="ExternalInput").ap(),
        lvm=nc.dram_tensor("lvm", [128, 7, 2, 128], F32, kind="ExternalInput").ap(),
        yacc=nc.dram_tensor("yacc", [T // 128, 128, 256], F32, kind="Internal").ap(),
    )
    out = nc.dram_tensor("rwkvT", [256, T], F32, kind="ExternalOutput").ap()
    S = Sched(nc)
    consts = _consts(nc, S)
    emit_rwkv(nc, S, T, d_in, out, consts)
    S.finish()
    return nc


def _attn_inputs(PT, gq, gk, hf):
    f = np.float32
    g0 = hf * TQ - HALO
    lo = max(g0, 0)
    hi = min(g0 + TK, SEQ)

    def padT(rows):
        out = np.zeros((512, TK), f)
        out[:, lo - g0:hi - g0] = rows[:, lo:hi]
        return out
    pos = g0 + np.arange(TK)
    valid = ((pos >= 0) & (pos < SEQ)).astype(f)
    vld = np.zeros((128, NKT), f)
    for kidx, (d, r, i, m0) in enumerate(KT):
        vld[:, kidx] = valid[r + d * (m0 + np.arange(128))]
    qkg = np.stack([np.tile(gq, 2), np.tile(gk, 2)], 1).astype(f)
    return dict(qT=np.ascontiguousarray(PT[0:512, hf * TQ:(hf + 1) * TQ]), kT=padT(PT[512:1024]), vT=padT(PT[1024:1536]), vld=vld, qkg=qkg)


def _rwkv_inputs(PT, hh, P):
    f = np.float32
    U = PT[1536:]
    cs = slice(hh * 256, (hh + 1) * 256)
    pad = lambda rows: np.pad(rows, ((0, 0), (1, 1)))
    u_rkv = np.stack([pad(U[0:512][cs]), pad(U[512:1024][cs]), pad(U[1024:1536][cs])], 0).astype(f)
    u_x = pad(U[1536:1792]).astype(f)
    sp, sn = P["sp"], P["sn"]
    cvec = np.zeros((128, 2, NCV), f)
    for j in range(2):
        ch = slice(hh * 256 + j * 128, hh * 256 + (j + 1) * 128)
        for q, key in enumerate(("spr", "spk", "spv")):
            a = sp[q * 512:(q + 1) * 512][ch]
            b = sn[q * 512:(q + 1) * 512][ch]
            cvec[:, j, CV[key]] = a
            cvec[:, j, CV[key] + 1] = b
            cvec[:, j, CV[key] + 2] = np.float32(1.0) - a - b
        cvec[:, j, CV["kk"]] = P["k_k"][ch]
        cvec[:, j, CV["ka"]] = P["k_a"][ch]
        cvec[:, j, CV["rk"]] = P["r_k"].reshape(-1)[ch]
        for d in range(2):
            cvec[:, j, CV["w00"] + d] = P["w0"][d][ch]
            cvec[:, j, CV["a00"] + d] = P["a0"][d][ch]
    xvec = np.zeros((128, 2, NXV), f)
    for i in range(2):
        a = sp[1536 + i * 128:1536 + (i + 1) * 128]
        b = sn[1536 + i * 128:1536 + (i + 1) * 128]
        xvec[:, i, 0] = a
        xvec[:, i, 1] = b
        xvec[:, i, 2] = np.float32(1.0) - a - b
    wa2 = np.zeros((128, 4, 256), f)
    for d in range(2):
        wa2[0:64, d] = P["w2"][d][:, cs]
        wa2[64:128, 2 + d] = P["a2"][d][:, cs]
    gnwb = np.stack([np.broadcast_to(P["gn_w"][cs], (128, 256)), np.broadcast_to(P["gn_b"][cs], (128, 256))], 1).astype(f)
    return dict(u_rkv=np.ascontiguousarray(u_rkv), u_x=np.ascontiguousarray(u_x), cvec=cvec, xvec=xvec, wa2=wa2,
                g2=np.ascontiguousarray(P["g2"][:, cs]).astype(f), gnwb=np.ascontiguousarray(gnwb), lvm=_level_masks())


_PROGS = {}


def _prog(name, fn):
    if name not in _PROGS:
        _PROGS[name] = fn()
    return _PROGS[name]


def kernel(x, ln1_g, w_in, q_norm_g, k_norm_g, tshift_prev, tshift_next, rwkv_w0, rwkv_w2,
           rwkv_a0, rwkv_a2, rwkv_g2, rwkv_k_k, rwkv_k_a, rwkv_r_k, rwkv_gn_w, rwkv_gn_b,
           w_out, ln2_g, w_up, w_down):
    f = np.float32
    A = lambda a: np.ascontiguousarray(np.asarray(a, f))
    x = A(x)
    ln1_g, ln2_g = A(ln1_g), A(ln2_g)
    cores = list(range(8))
    gcol = lambda g: g.reshape(8, 128).T
    xT = [np.ascontiguousarray(x[c // 2, (c % 2) * TOK:(c % 2 + 1) * TOK, :].T) for c in cores]

    gv = np.ascontiguousarray(np.stack([gcol(ln2_g[0]), gcol(ln1_g[0])], 1))
    res = run_bass_kernel_spmd(_prog("tokP", lambda: build_tok(False, True)),
                               [dict(xT=xT[c], gv=gv, w_in=A(w_in[0])) for c in cores], core_ids=cores)
    projT = [np.asarray(res.results[c]["projT"], f) for c in cores]

    for l in range(DEPTH):
        PT = [np.concatenate([projT[2 * b], projT[2 * b + 1]], axis=1) for b in range(NB)]
        gq, gk = A(q_norm_g[l]), A(k_norm_g[l])
        res = run_bass_kernel_spmd(_prog("attn", build_attn), [_attn_inputs(PT[c // 2], gq, gk, c % 2) for c in cores], core_ids=cores)
        attnT = [np.asarray(res.results[c]["attnT"], f) for c in cores]
        P = dict(sp=A(tshift_prev[l]), sn=A(tshift_next[l]), w0=A(rwkv_w0[l]), w2=A(rwkv_w2[l]), a0=A(rwkv_a0[l]), a2=A(rwkv_a2[l]),
                 g2=A(rwkv_g2[l]), k_k=A(rwkv_k_k[l]), k_a=A(rwkv_k_a[l]), r_k=A(rwkv_r_k[l]), gn_w=A(rwkv_gn_w[l]), gn_b=A(rwkv_gn_b[l]))
        res = run_bass_kernel_spmd(_prog("rwkv", lambda: build_rwkv(SEQ)), [_rwkv_inputs(PT[c // 2], c % 2, P) for c in cores], core_ids=cores)
        rwkvT = [np.asarray(res.results[c]["rwkvT"], f) for c in cores]
        last = (l == DEPTH - 1)
        ins = []
        for c in cores:
            b, hf = c // 2, c % 2
            ts = slice(hf * TOK, (hf + 1) * TOK)
            catT = np.concatenate([attnT[c], rwkvT[2 * b][:, ts], rwkvT[2 * b + 1][:, ts]], axis=0)
            gv = np.ascontiguousarray(np.stack([gcol(ln2_g[l]), gcol(ln1_g[min(l + 1, DEPTH - 1)])], 1))
            d = dict(xT=xT[c], gv=gv, catT=np.ascontiguousarray(catT), w_out=A(w_out[l]), w_up=A(w_up[l]), w_down=A(w_down[l]))
            if not last:
                d["w_in"] = A(w_in[l + 1])
            ins.append(d)
        if last:
            res = run_bass_kernel_spmd(_prog("tokM", lambda: build_tok(True, False)), ins, core_ids=cores)
        else:
            res = run_bass_kernel_spmd(_prog("tokMP", lambda: build_tok(True, True)), ins, core_ids=cores)
            projT = [np.asarray(res.results[c]["projT"], f) for c in cores]
        xT = [np.asarray(res.results[c]["xT_out"], f) for c in cores]

    out = np.empty((NB, SEQ, D), f)
    for c in cores:
        out[c // 2, (c % 2) * TOK:(c % 2 + 1) * TOK, :] = xT[c].T
    return out
```
